# Optimizing a Trainium2 kernel written in Bass

```python
import jax, jax.numpy as jnp
from jax import lax
import numpy as np

D_MODEL = 2048
BATCH = 4
SEQ = 2048
DEPTH = 2

CHUNK = 64
Q_BLOCK = 128
MLA_HEADS = 8
QK_NOPE = 128
QK_ROPE = 64
V_HEAD = 128
Q_LORA = 512
KV_LORA = 256
MLA_WIDTH = MLA_HEADS * V_HEAD
CONV_WIDTH = D_MODEL - MLA_WIDTH
CONV_GROUPS = 8
CONV_K = 31
D_FF = 4 * D_MODEL
ROPE_THETA = 10000.0
LN_EPS = 1e-5
RMS_EPS = 1e-6
ALPHA = (2.0 * DEPTH) ** 0.25
BETA = (8.0 * DEPTH) ** -0.25
IN_COLS = Q_LORA + KV_LORA + QK_ROPE + 2 * CONV_WIDTH

kernel_name = "mla_conformer_conv_hybrid_deepnorm"


def layer_norm(x, g, b):
    xf = x.astype(jnp.float32)
    mu = jnp.mean(xf, axis=-1, keepdims=True)
    var = jnp.mean(jnp.square(xf - mu), axis=-1, keepdims=True)
    y = (xf - mu) * lax.rsqrt(var + LN_EPS)
    return (y * g.astype(jnp.float32) + b.astype(jnp.float32)).astype(x.dtype)


def rms_norm(x, g):
    xf = x.astype(jnp.float32)
    y = xf * lax.rsqrt(jnp.mean(jnp.square(xf), axis=-1, keepdims=True) + RMS_EPS)
    return (y * g.astype(jnp.float32)).astype(x.dtype)


def rope_tables(positions):
    inv_freq = ROPE_THETA ** (-jnp.arange(0, QK_ROPE, 2, dtype=jnp.float32) / QK_ROPE)
    ang = positions.astype(jnp.float32)[..., None] * inv_freq
    return jnp.cos(ang), jnp.sin(ang)


def apply_rope(x, cos, sin):
    xf = x.astype(jnp.float32)
    half = xf.shape[-1] // 2
    x1, x2 = xf[..., :half], xf[..., half:]
    return jnp.concatenate([x1 * cos - x2 * sin, x2 * cos + x1 * sin], axis=-1).astype(x.dtype)


def chunk_causal_mla(q_nope, q_rope, k_nope, k_rope, v):
    S = q_nope.shape[1]
    scale = (QK_NOPE + QK_ROPE) ** -0.5
    outs = []
    for blk in range(S // Q_BLOCK):
        q0 = blk * Q_BLOCK
        kend = q0 + Q_BLOCK
        s = (jnp.einsum('bqhd,bkhd->bhqk', q_nope[:, q0:kend], k_nope[:, :kend])
             + jnp.einsum('bqhr,bkr->bhqk', q_rope[:, q0:kend], k_rope[:, :kend]))
        s = s.astype(jnp.float32) * scale
        q_chunk = (q0 + jnp.arange(Q_BLOCK)) // CHUNK
        k_chunk = jnp.arange(kend) // CHUNK
        mask = k_chunk[None, :] <= q_chunk[:, None]
        s = jnp.where(mask, s, -1e30)
        p = jax.nn.softmax(s, axis=-1).astype(v.dtype)
        outs.append(jnp.einsum('bhqk,bkhd->bqhd', p, v[:, :kend]))
    return jnp.concatenate(outs, axis=1)


def causal_depthwise_conv(u, w, b):
    C = u.shape[-1]
    y = lax.conv_general_dilated(
        u, w[:, None, :].astype(u.dtype), window_strides=(1,),
        padding=((CONV_K - 1, 0),), dimension_numbers=('NWC', 'WIO', 'NWC'),
        feature_group_count=C)
    return y + b.astype(u.dtype)


def hybrid_layer(x, cos, sin, w_in, g_q, w_uq, g_kv, w_ukv, b_glu, w_dw, b_dw,
                 g_cln, b_cln, w_out, ln1_g, ln1_b, w1, w2, ln2_g, ln2_b):
    B, S, _ = x.shape
    z = x @ w_in
    o1 = Q_LORA
    o2 = o1 + KV_LORA
    o3 = o2 + QK_ROPE
    c_q, c_kv, k_rope, u = z[..., :o1], z[..., o1:o2], z[..., o2:o3], z[..., o3:]

    q = (rms_norm(c_q, g_q) @ w_uq).reshape(B, S, MLA_HEADS, QK_NOPE + QK_ROPE)
    q_nope = q[..., :QK_NOPE]
    q_rope = apply_rope(q[..., QK_NOPE:], cos[:, :, None, :], sin[:, :, None, :])
    kv = (rms_norm(c_kv, g_kv) @ w_ukv).reshape(B, S, MLA_HEADS, QK_NOPE + V_HEAD)
    k_nope, v = kv[..., :QK_NOPE], kv[..., QK_NOPE:]
    k_rope = apply_rope(k_rope, cos, sin)
    attn = chunk_causal_mla(q_nope, q_rope, k_nope, k_rope, v).reshape(B, S, MLA_WIDTH)

    u = u + b_glu
    h = u[..., :CONV_WIDTH] * jax.nn.sigmoid(u[..., CONV_WIDTH:])
    h = causal_depthwise_conv(h, w_dw, b_dw)
    h = jax.nn.silu(layer_norm(h, g_cln, b_cln))

    mix = jnp.concatenate([attn, h], axis=-1) @ w_out
    x = layer_norm(ALPHA * x + mix, ln1_g, ln1_b)

    f = jnp.square(jax.nn.relu(x @ w1)) @ w2
    return layer_norm(ALPHA * x + f, ln2_g, ln2_b)


def setup_inputs(seed: int = 0) -> dict:
    key = jax.random.key(seed)
    ks = jax.random.split(key, 24)
    f32 = jnp.float32
    nrm = lambda k, shape, s: jax.random.normal(k, shape, f32) * s
    L = DEPTH
    x = jax.random.normal(ks[0], (BATCH, SEQ, D_MODEL), f32)
    start = jax.random.randint(ks[1], (BATCH, 1), 0, 4096, dtype=jnp.int32)
    positions = (start + jnp.arange(SEQ, dtype=jnp.int32)[None, :]).astype(jnp.int32)
    v_scale = jnp.tile(jnp.concatenate([jnp.ones((QK_NOPE,), f32),
                                        jnp.full((V_HEAD,), BETA, f32)]), MLA_HEADS)
    return {
        "x": x,
        "positions": positions,
        "ln_in_g": 1.0 + nrm(ks[2], (D_MODEL,), 0.02),
        "ln_in_b": nrm(ks[3], (D_MODEL,), 0.02),
        "w_in": nrm(ks[4], (L, D_MODEL, IN_COLS), D_MODEL ** -0.5),
        "g_q": 1.0 + nrm(ks[5], (L, Q_LORA), 0.02),
        "w_uq": nrm(ks[6], (L, Q_LORA, MLA_HEADS * (QK_NOPE + QK_ROPE)), Q_LORA ** -0.5),
        "g_kv": 1.0 + nrm(ks[7], (L, KV_LORA), 0.02),
        "w_ukv": nrm(ks[8], (L, KV_LORA, MLA_HEADS * (QK_NOPE + V_HEAD)), KV_LORA ** -0.5) * v_scale,
        "b_glu": nrm(ks[9], (L, 2 * CONV_WIDTH), 0.02),
        "w_dw": nrm(ks[10], (L, CONV_K, CONV_WIDTH), CONV_K ** -0.5),
        "b_dw": nrm(ks[11], (L, CONV_WIDTH), 0.02),
        "g_cln": 1.0 + nrm(ks[12], (L, CONV_WIDTH), 0.02),
        "b_cln": nrm(ks[13], (L, CONV_WIDTH), 0.02),
        "w_out": nrm(ks[14], (L, D_MODEL, D_MODEL), BETA * D_MODEL ** -0.5),
        "ln1_g": 1.0 + nrm(ks[15], (L, D_MODEL), 0.02),
        "ln1_b": nrm(ks[16], (L, D_MODEL), 0.02),
        "w1": nrm(ks[17], (L, D_MODEL, D_FF), BETA * D_MODEL ** -0.5),
        "w2": nrm(ks[18], (L, D_FF, D_MODEL), BETA * D_FF ** -0.5),
        "ln2_g": 1.0 + nrm(ks[19], (L, D_MODEL), 0.02),
        "ln2_b": nrm(ks[20], (L, D_MODEL), 0.02),
    }


def reference(x, positions, ln_in_g, ln_in_b, w_in, g_q, w_uq, g_kv, w_ukv, b_glu,
              w_dw, b_dw, g_cln, b_cln, w_out, ln1_g, ln1_b, w1, w2, ln2_g, ln2_b):
    cos, sin = rope_tables(positions)
    h = layer_norm(x, ln_in_g, ln_in_b)
    for l in range(DEPTH):
        h = hybrid_layer(h, cos, sin, w_in[l], g_q[l], w_uq[l], g_kv[l], w_ukv[l],
                         b_glu[l], w_dw[l], b_dw[l], g_cln[l], b_cln[l], w_out[l],
                         ln1_g[l], ln1_b[l], w1[l], w2[l], ln2_g[l], ln2_b[l])
    return h
```

```python
import numpy as np
import ml_dtypes
import concourse.bass as bass
import concourse.mybir as mybir
from concourse.bass_utils import run_bass_kernel_spmd
from contextlib import ExitStack

F32 = mybir.dt.float32
BF16 = mybir.dt.bfloat16
I32 = mybir.dt.int32
AF = mybir.ActivationFunctionType
ALU = mybir.AluOpType
AX = mybir.AxisListType

D = 2048
T = 1024
NBLK = 8
DEPTH = 2
QL, KVL, ROPE = 512, 256, 64
NH = 8
CW = 1024
CK = 31
DFF = 8192
IN_COLS = QL + KVL + ROPE + 2 * CW
LN_EPS = 1e-5
RMS_EPS = 1e-6
ALPHA = (2.0 * DEPTH) ** 0.25
SCALE = (128 + 64) ** -0.5
TWO_PI = 2.0 * np.pi

ENGS = ["pe", "dve", "act", "pool", "sp"]
SAME_ENGINE_SYNC = {"dve", "act", "pool"}


class Prog:
    def __init__(self, nc, es):
        self.nc = nc
        self.es = es
        self.ops = {e: [] for e in ENGS}
        self.cnt = {e: 0 for e in ENGS}
        self.sems = {"E:" + e: es.enter_context(nc.semaphore("sem_" + e)) for e in ENGS}
        self.dcnt = {}
        self.last_w = {}
        self.readers = {}
        self.seen = {e: {} for e in ENGS}

    def _dsem(self, key):
        name = "D:" + key
        if name not in self.sems:
            self.sems[name] = self.es.enter_context(self.nc.semaphore("dsem_" + key))
            self.dcnt[name] = 0
        return name

    def op(self, eng, fns, reads=(), writes=(), dma=None, inc=16):
        if not isinstance(fns, (list, tuple)):
            fns = [fns]
        deps = set()
        for k in reads:
            if k in self.last_w:
                deps.add(self.last_w[k])
        for k in writes:
            if k in self.last_w:
                deps.add(self.last_w[k])
            for ev in self.readers.get(k, ()):
                deps.add(ev)
        need = {}
        for (s, v) in deps:
            if v > need.get(s, 0):
                need[s] = v
        waits = []
        for s, v in need.items():
            if s == "E:" + eng and dma is None and eng not in SAME_ENGINE_SYNC:
                continue
            if self.seen[eng].get(s, 0) >= v:
                continue
            self.seen[eng][s] = v
            waits.append((s, v))
        if dma is None:
            self.cnt[eng] += 1
            ev = ("E:" + eng, self.cnt[eng])
            inc = 1
        else:
            name = self._dsem(dma)
            self.dcnt[name] += inc * len(fns)
            ev = (name, self.dcnt[name])
        for k in writes:
            self.last_w[k] = ev
            self.readers[k] = []
        for k in reads:
            if k not in writes:
                self.readers.setdefault(k, []).append(ev)
        self.ops[eng].append((waits, fns, ev[0], inc, dma is not None))
        return ev

    def barrier(self, final=False):
        evs = [("E:" + e, self.cnt[e]) for e in ENGS if self.cnt[e] > 0]
        evs += [(n, v) for n, v in self.dcnt.items() if v > 0 and (final or not n.startswith("D:wslot"))]
        for e in ENGS:
            waits = []
            for (s, v) in evs:
                if s == "E:" + e:
                    continue
                if self.seen[e].get(s, 0) >= v:
                    continue
                self.seen[e][s] = v
                waits.append((s, v))
            if waits:
                self.ops[e].append((waits, [], None, 0, False))
        self.last_w = {k_: v for k_, v in self.last_w.items() if k_.startswith("wslot")}
        self.readers = {k_: v for k_, v in self.readers.items() if k_.startswith("wslot")}

    def emit(self):
        nc = self.nc
        with nc.Block() as block:
            def run(e, eng):
                for (waits, fns, sname, inc, is_dma) in self.ops[e]:
                    for (s, v) in waits:
                        eng.wait_ge(self.sems[s], v)
                    for i, f in enumerate(fns):
                        ins = f(eng)
                        if is_dma or i == len(fns) - 1:
                            ins.then_inc(self.sems[sname], inc)

            @block.tensor
            def _(eng):
                run("pe", eng)

            @block.vector
            def _(eng):
                run("dve", eng)

            @block.scalar
            def _(eng):
                run("act", eng)

            @block.gpsimd
            def _(eng):
                run("pool", eng)

            @block.sync
            def _(eng):
                run("sp", eng)


def _vec_layout():
    g = {}
    off = 0
    for name, n in [("ln_in_g", 16), ("ln_in_b", 16), ("inv_freq", 1), ("sgn", 1), ("s_a", 1), ("s_b", 1)]:
        g[name] = off
        off += n
    gl = off
    l = {}
    off = 0
    for name, n in [("g_q", 4), ("g_kv", 2), ("b_glu", 16), ("w_dw", 8 * CK), ("b_dw", 8), ("g_cln", 8),
                    ("b_cln", 8), ("ln1_g", 16), ("ln1_b", 16), ("ln2_g", 16), ("ln2_b", 16)]:
        l[name] = off
        off += n
    return g, gl, l, off


VG, NVG, VL, NVL = _vec_layout()
NV = NVG + DEPTH * NVL


class K:
    def __init__(self, nc, es, mode):
        self.nc = nc
        self.es = es
        self.P = Prog(nc, es)
        self.mode = mode
        self.POOLW = 53000
        self.pool = es.enter_context(nc.sbuf_tensor("pool", [128, self.POOLW], F32))
        self.ps = es.enter_context(nc.psum_tensor("ps", [128, 4096], F32))
        self.off = 0
        self.base = 0
        self.uid = 0
        self.rr = 0
        self.ring = None

    def alloc(self, nelem, dtype, parts=128):
        nbytes = nelem * (2 if dtype == BF16 else 4)
        words = (nbytes + 31) // 32 * 8
        assert self.off + words <= self.POOLW, ("SBUF pool overflow", self.off, words)
        a = self.pool[0:parts, self.off:self.off + words]
        self.off += words
        if dtype != F32:
            a = a.bitcast(dtype)
        a = a[:, 0:nelem]
        self.uid += 1
        return a, f"b{self.uid}"

    def phase(self):
        self.P.barrier()
        self.off = self.base

    def bank(self, b=None):
        if b is None:
            b = self.rr
            self.rr = (self.rr + 1) % 8
        return self.ps[:, b * 512:(b + 1) * 512], f"ps{b}"

    def dma(self, eng, out, in_, reads, writes, key):
        return self.P.op(eng, lambda e: e.dma_start(out=out, in_=in_), reads=reads, writes=writes, dma=key)

    def mm(self, out, pairs, reads, writes, first=True, last=True):
        n = len(pairs)
        fns = []
        for i, (l, r) in enumerate(pairs):
            fns.append(lambda e, l=l, r=r, i=i: e.matmul(out, lhsT=l, rhs=r, start=(first and i == 0),
                                                         stop=(last and i == n - 1)))
        return self.P.op("pe", fns, reads=reads, writes=writes)

    def act(self, out, in_, func, reads, writes, scale=1.0, bias=0.0):
        return self.P.op("act", lambda e: e.activation(out=out, in_=in_, func=func, bias=bias, scale=scale),
                         reads=reads, writes=writes)

    def tt(self, eng, out, in0, in1, op, reads, writes):
        return self.P.op(eng, lambda e: e.tensor_tensor(out=out, in0=in0, in1=in1, op=op), reads=reads, writes=writes)

    def ts(self, eng, out, in0, s1, s2, op0, op1, reads, writes):
        if op1 is None:
            return self.P.op(eng, lambda e: e.tensor_scalar(out=out, in0=in0, scalar1=s1, scalar2=None, op0=op0),
                             reads=reads, writes=writes)
        return self.P.op(eng, lambda e: e.tensor_scalar(out=out, in0=in0, scalar1=s1, scalar2=s2, op0=op0, op1=op1),
                         reads=reads, writes=writes)

    def stt(self, out, in0, scalar, in1, op0, op1, reads, writes):
        return self.P.op("dve", lambda e: e.scalar_tensor_tensor(out=out, in0=in0, scalar=scalar, in1=in1,
                                                                 op0=op0, op1=op1), reads=reads, writes=writes)

    def ring_setup(self, sched, nslots=4, slot_elems=8192):
        self.ring = [self.alloc(slot_elems, BF16) for _ in range(nslots)]
        self.wsched = sched
        self.w_issued = 0
        self.w_taken = 0

    def _wissue(self):
        i = self.w_issued
        src2d, nkc, ncols = self.wsched[i]
        ap, _ = self.ring[i % len(self.ring)]
        key = f"wslot{i % len(self.ring)}"
        view = ap[:, 0:nkc * ncols].rearrange("p (c n) -> p c n", c=nkc)
        src = src2d.rearrange("(c p) n -> p c n", p=128)
        self.dma("pool", view, src, [], [key], key)
        self.w_issued += 1

    def ring_init(self, *a, **kw):
        pass

    def wload(self, src2d, nkc, ncols):
        c = self.w_taken
        s2, n2, c2 = self.wsched[c]
        assert (n2, c2) == (nkc, ncols), ("weight schedule mismatch", c)
        while self.w_issued < min(len(self.wsched), c + len(self.ring) - 1):
            self._wissue()
        ap, _ = self.ring[c % len(self.ring)]
        key = f"wslot{c % len(self.ring)}"
        self.w_taken += 1
        return ap[:, 0:nkc * ncols].rearrange("p (c n) -> p c n", c=nkc), key


def load_consts(k, vecs_d, cst_d, nv):
    k.vecs, k.vk = k.alloc(nv, F32)
    k.dma("sp", k.vecs, vecs_d, [], [k.vk], "vecs")
    cst, ck = k.alloc(256, F32)
    k.dma("sp", cst, cst_d, [], [ck], "cst")
    k.ident_f, k.ones_f, k.ck = cst[:, 0:128], cst[:, 128:256], ck
    k.ident_b, k.ibk = k.alloc(128, BF16)
    k.P.op("dve", lambda e: e.tensor_copy(out=k.ident_b, in_=k.ident_f), reads=[ck], writes=[k.ibk])
    k.base = k.off


def vcol_idx(k, name, layer):
    if name in VG:
        return VG[name]
    return NVG + (layer if k.mode == "FUSED" else 0) * NVL + VL[name]


def vcol(k, name, layer, i=0):
    if name in VG:
        c = VG[name] + i
    else:
        c = NVG + (layer if k.mode == "FUSED" else 0) * NVL + VL[name] + i
    return k.vecs[:, c:c + 1]


def rstd_from_sums(k, psA, ka, psB, kb, n, eps, tag):
    rstd, rk = k.alloc(512, F32)
    mean, mk = (None, None)
    if psA is not None:
        mean, mk = k.alloc(512, F32)
        msq, qk = k.alloc(512, F32)
        k.act(mean, psA, AF.Copy, [ka], [mk], scale=1.0 / n)
        k.act(msq, mean, AF.Square, [mk], [qk])
        k.stt(rstd, psB, 1.0 / n, msq, ALU.mult, ALU.subtract, [kb, qk], [rk])
        k.act(rstd, rstd, AF.Sqrt, [rk], [rk], bias=eps)
    else:
        k.act(rstd, psB, AF.Sqrt, [kb], [rk], scale=1.0 / n, bias=eps)
    k.P.op("dve", lambda e: e.reciprocal(out=rstd, in_=rstd), reads=[rk], writes=[rk])
    return mean, mk, rstd, rk


def lnfm(k, src_d, sk, C, gname, bname, layer, eps, func, dst_f=None, dfk=None, dst_b=None, dbk=None, res=None):
    P = k.P
    N = C * 128
    NT = T // 512
    if res is None:
        xs = [[k.alloc(4 * 512, F32) for _ in range(C // 4)] for _ in range(NT)]
    sq = [k.alloc(512, F32) for _ in range(2)]

    def chunk(c, t):
        if res is not None:
            return res[0][:, c, t * 512:(t + 1) * 512], res[1][c]
        a_, ak_ = xs[t][c // 4]
        return a_[:, (c % 4) * 512:(c % 4 + 1) * 512], ak_

    stats = []
    n = 0
    for t in range(NT):
        tsl = slice(t * 512, (t + 1) * 512)
        psA, ka = k.bank()
        psB, kb = k.bank()
        for c in range(C):
            x, xk = chunk(c, t)
            if res is None and c % 4 == 0:
                k.dma("sp", xs[t][c // 4][0].rearrange("p (c n) -> p c n", c=4),
                      src_d[c * 128:(c + 4) * 128, tsl].rearrange("(c p) n -> p c n", p=128), [sk], [xk], f"lnx{t}_{c//4}")
            s, sqk = sq[n % 2]
            n += 1
            k.act(s, x, AF.Square, [xk], [sqk])
            k.mm(psA, [(k.ones_f, x)], [k.ck, xk], [ka], first=(c == 0), last=(c == C - 1))
            k.mm(psB, [(k.ones_f, s)], [k.ck, sqk], [kb], first=(c == 0), last=(c == C - 1))
        stats.append(rstd_from_sums(k, psA, ka, psB, kb, N, eps, "ln"))
    n = 0
    o16g = [k.alloc(4 * 512, BF16) for _ in range(3)] if dst_b is not None else None
    for t in range(NT):
        tsl = slice(t * 512, (t + 1) * 512)
        mean, mk, rstd, rk = stats[t]
        for c0 in range(0, C, 4):
            grp = [chunk(c, t) for c in range(c0, c0 + 4)]
            for (x, xk) in grp:
                k.tt("dve", x, x, mean, ALU.subtract, [xk, mk], [xk])
            for (x, xk) in grp:
                k.tt("dve", x, x, rstd, ALU.mult, [xk, rk], [xk])
            if dst_f is not None:
                if dst_b is not None:
                    ob, obk = o16g[n % 3]
                    for i, (x, xk) in enumerate(grp):
                        k.act(ob[:, i * 512:(i + 1) * 512], x, func, [xk, k.vk], [obk],
                              scale=vcol(k, gname, layer, c0 + i), bias=vcol(k, bname, layer, c0 + i))
                    k.dma("sp", dst_b[c0 * 128:(c0 + 4) * 128, tsl].rearrange("(c p) n -> p c n", p=128),
                          ob.rearrange("p (c n) -> p c n", c=4), [obk], [dbk], f"lnsb{n%4}")
                for i, (x, xk) in enumerate(grp):
                    k.act(x, x, func, [xk, k.vk], [xk], scale=vcol(k, gname, layer, c0 + i), bias=vcol(k, bname, layer, c0 + i))
                if res is not None:
                    src = res[0][:, c0:c0 + 4, tsl]
                else:
                    src = xs[t][c0 // 4][0].rearrange("p (c n) -> p c n", c=4)
                k.dma("sp", dst_f[c0 * 128:(c0 + 4) * 128, tsl].rearrange("(c p) n -> p c n", p=128), src,
                      list({xk for (_, xk) in grp}), [dfk], f"lnst{n%4}")
            else:
                ob, obk = o16g[n % 3]
                for i, (x, xk) in enumerate(grp):
                    k.act(ob[:, i * 512:(i + 1) * 512], x, func, [xk, k.vk], [obk], scale=vcol(k, gname, layer, c0 + i),
                          bias=vcol(k, bname, layer, c0 + i))
                k.dma("sp", dst_b[c0 * 128:(c0 + 4) * 128, tsl].rearrange("(c p) n -> p c n", p=128),
                      ob.rearrange("p (c n) -> p c n", c=4), [obk], [dbk], f"lnst{n%4}")
            n += 1


def ln_scratch(k):
    NT = T // 512
    return {"sq": [k.alloc(512, F32) for _ in range(2)],
            "mean": [k.alloc(512, F32) for _ in range(NT)],
            "rstd": [k.alloc(512, F32) for _ in range(NT)]}


def ln_stats(k, chunk, C, eps, scr, loader=None):
    N = C * 128
    stats = []
    n = 0
    for t in range(T // 512):
        psA, ka = k.bank()
        psB, kb = k.bank()
        for c in range(C):
            if loader is not None:
                loader(c, t)
            x, xk = chunk(c, t)
            s, sqk = scr["sq"][n % 2]
            n += 1
            k.act(s, x, AF.Square, [xk], [sqk])
            k.mm(psA, [(k.ones_f, x)], [k.ck, xk], [ka], first=(c == 0), last=(c == C - 1))
            k.mm(psB, [(k.ones_f, s)], [k.ck, sqk], [kb], first=(c == 0), last=(c == C - 1))
        mean, mk = scr["mean"][t]
        rstd, rk = scr["rstd"][t]

        def fin(psA=psA, ka=ka, psB=psB, kb=kb, mean=mean, mk=mk, rstd=rstd, rk=rk):
            k.act(mean, psA, AF.Copy, [ka], [mk], scale=1.0 / N)
            k.act(rstd, mean, AF.Square, [mk], [rk])
            k.stt(rstd, psB, 1.0 / N, rstd, ALU.mult, ALU.subtract, [kb, rk], [rk])
            k.act(rstd, rstd, AF.Sqrt, [rk], [rk], bias=eps)
            k.P.op("dve", lambda e, rstd=rstd: e.reciprocal(out=rstd, in_=rstd), reads=[rk], writes=[rk])

        if t == T // 512 - 1:
            k.ln_fin = fin
        else:
            fin()
        stats.append((mean, mk, rstd, rk))
    return stats


def ln_norm(k, chunk, stats, C, epilogue):
    pending = [None]
    for t in range(T // 512):
        if t == T // 512 - 1 and getattr(k, "ln_fin", None) is not None:
            k.ln_fin()
            k.ln_fin = None
        mean, mk, rstd, rk = stats[t]
        for c0 in range(0, C, 4):
            grp = [chunk(c, t) for c in range(c0, c0 + 4)]
            for (x, xk) in grp:
                k.tt("dve", x, x, mean, ALU.subtract, [xk, mk], [xk])
            for (x, xk) in grp:
                k.tt("dve", x, x, rstd, ALU.mult, [xk, rk], [xk])
            if pending[0] is not None:
                pending[0]()
            pending[0] = epilogue(t, c0, grp)
    if pending[0] is not None:
        pending[0]()


def rope_tables(k, pos_d):
    if getattr(k, "Ct", None) is not None:
        return
    P = k.P
    Ct, Ck = k.alloc(T, F32, parts=64)
    St, Sk = k.alloc(T, F32, parts=64)
    k.base = k.off
    posi, pk = k.alloc(T, I32, parts=64)
    k.dma("sp", posi, pos_d, [], [pk], "posi")
    ang, ak = k.alloc(T, F32, parts=64)
    tmp, tk = k.alloc(T, F32, parts=64)
    ki, kik = k.alloc(T, I32, parts=64)
    invf = k.vecs[0:64, VG["inv_freq"]:VG["inv_freq"] + 1]
    sgn = k.vecs[0:64, VG["sgn"]:VG["sgn"] + 1]
    P.op("dve", lambda e: e.tensor_copy(out=ang, in_=posi), reads=[pk], writes=[ak])
    k.ts("dve", ang, ang, invf, None, ALU.mult, None, [ak, k.vk], [ak])

    def reduce_sin(dst, dk, shift, scale_ap):
        k.ts("dve", tmp, ang, shift, 1.0 / TWO_PI, ALU.add, ALU.mult, [ak], [tk])
        P.op("dve", lambda e: e.tensor_copy(out=ki, in_=tmp), reads=[tk], writes=[kik])
        P.op("dve", lambda e: e.tensor_copy(out=tmp, in_=ki), reads=[kik], writes=[tk])
        k.ts("dve", tmp, tmp, -TWO_PI, shift, ALU.mult, ALU.add, [tk], [tk])
        k.tt("dve", tmp, tmp, ang, ALU.add, [tk, ak], [tk])
        k.ts("dve", dst, tmp, float(np.pi), -TWO_PI, ALU.is_gt, ALU.mult, [tk], [dk])
        k.tt("dve", tmp, tmp, dst, ALU.add, [tk, dk], [tk])
        k.ts("dve", dst, tmp, float(-np.pi), TWO_PI, ALU.is_lt, ALU.mult, [tk], [dk])
        k.tt("dve", tmp, tmp, dst, ALU.add, [tk, dk], [tk])
        k.ts("dve", tmp, tmp, 3.14159, -3.14159, ALU.min, ALU.max, [tk], [tk])
        k.act(dst, tmp, AF.Sin, [tk, k.vk], [dk], scale=scale_ap)

    reduce_sin(St, Sk, 0.0, sgn)
    reduce_sin(Ct, Ck, float(np.pi / 2), 1.0)
    k.Ct, k.Ck, k.St, k.Sk = Ct, Ck, St, Sk


def rope_scratch(k, n):
    return k.alloc(n, F32, parts=64), k.alloc(n, F32, parts=64)


def rope_apply(k, xf, xfk, dst, dk, sl, prescale, tag, scr):
    (sw, swk), (t1, t1k) = scr
    k.dma("sp", sw[0:32, :], xf[32:64, :], [xfk], [swk], f"sw{tag}a")
    k.dma("sp", sw[32:64, :], xf[0:32, :], [xfk], [swk + "b"], f"sw{tag}b")
    k.stt(t1, xf, prescale, k.Ct[:, sl], ALU.mult, ALU.mult, [xfk, k.Ck], [t1k])
    k.stt(sw, sw, prescale, k.St[:, sl], ALU.mult, ALU.mult, [swk, swk + "b", k.Sk], [swk, swk + "b"])
    k.tt("dve", dst, t1, sw, ALU.add, [t1k, swk, swk + "b"], [dk])


def phase_a(k, layer, XT_d, xtk, pos_d, w_in, cqn_d, lat_own, h_d, halo_own, x_in_sbuf=False):
    P = k.P
    k.phase()
    rope_tables(k, pos_d)
    xt, _ = k.alloc(16 * T, BF16)
    xt = xt.rearrange("p (c t) -> p c t", c=16)
    xks = []
    for q in range(0 if x_in_sbuf else 4):
        kk = f"xtq{q}"
        k.dma("sp", xt[:, q * 4:(q + 1) * 4, :], XT_d[q * 512:(q + 1) * 512, :].rearrange("(c p) t -> p c t", p=128),
              [xtk], [kk], kk)
        xks.append(kk)
    k.ring_init(4, 8192)

    cf = [k.alloc(512, F32) for _ in range(4)]
    sq = [k.alloc(512, F32) for _ in range(2)]
    ob = [k.alloc(512, BF16) for _ in range(2)]

    def norm_proj(panel, pkey, col0, noc, gname, nfeat, dst_d, dkey, row0):
        for t in range(2):
            tsl = slice(t * 512, (t + 1) * 512)
            psB, kb = k.bank()
            for oc in range(noc):
                ps, pk = k.bank()
                k.mm(ps, [(panel[:, kc, col0 + oc * 128:col0 + (oc + 1) * 128], xt[:, kc, tsl]) for kc in range(16)],
                     [pkey] + xks, [pk])
                c, ckk = cf[oc]
                k.act(c, ps, AF.Copy, [pk], [ckk])
                s, sk = sq[oc % 2]
                k.act(s, ps, AF.Square, [pk], [sk])
                k.mm(psB, [(k.ones_f, s)], [k.ck, sk], [kb], first=(oc == 0), last=(oc == noc - 1))
            _, _, rstd, rk = rstd_from_sums(k, None, None, psB, kb, nfeat, RMS_EPS, "rms")
            for oc in range(noc):
                c, ckk = cf[oc]
                o, ok = ob[oc % 2]
                k.stt(o, c, vcol(k, gname, layer, oc), rstd, ALU.mult, ALU.mult, [ckk, rk, k.vk], [ok])
                k.dma("sp", dst_d[row0 + oc * 128:row0 + (oc + 1) * 128, tsl], o, [ok], [dkey], f"npo{oc%2}")

    pan, pkey = k.wload(w_in[:, 0:512], 16, 512)
    norm_proj(pan, pkey, 0, 4, "g_q", QL, cqn_d, "cqn_d", 0)
    pan, pkey = k.wload(w_in[:, 512:832], 16, 320)
    norm_proj(pan, pkey, 0, 2, "g_kv", KVL, lat_own, "lat_own", 0)
    kscr = rope_scratch(k, 512)
    krf, kfk = k.alloc(512, F32, parts=64)
    krb, kbk = k.alloc(512, BF16, parts=64)
    for t in range(2):
        tsl = slice(t * 512, (t + 1) * 512)
        ps, pk = k.bank()
        k.mm(ps[0:64, :], [(pan[:, kc, 256:320], xt[:, kc, tsl]) for kc in range(16)], [pkey] + xks, [pk])
        k.act(krf, ps[0:64, :], AF.Copy, [pk], [kfk])
        rope_apply(k, krf, kfk, krb, kbk, tsl, 1.0, "k", kscr)
        k.dma("sp", lat_own[256:320, tsl], krb, [kbk], ["lat_own"], "krb")
    hs = [k.alloc(512, F32) for _ in range(2)]
    sg = [k.alloc(512, F32) for _ in range(2)]
    n = 0
    for hp in range(2):
        pa, pak = k.wload(w_in[:, 832 + hp * 512:832 + (hp + 1) * 512], 16, 512)
        pg, pgk = k.wload(w_in[:, 832 + 1024 + hp * 512:832 + 1024 + (hp + 1) * 512], 16, 512)
        for c4 in range(4):
            cc = hp * 4 + c4
            for t in range(2):
                tsl = slice(t * 512, (t + 1) * 512)
                psa, ka = k.bank()
                psg, kg = k.bank()
                k.mm(psa, [(pa[:, kc, c4 * 128:(c4 + 1) * 128], xt[:, kc, tsl]) for kc in range(16)], [pak] + xks, [ka])
                k.mm(psg, [(pg[:, kc, c4 * 128:(c4 + 1) * 128], xt[:, kc, tsl]) for kc in range(16)], [pgk] + xks, [kg])
                s, sk = sg[n % 2]
                h, hk = hs[n % 2]
                k.act(s, psg, AF.Sigmoid, [kg, k.vk], [sk], bias=vcol(k, "b_glu", layer, 8 + cc))
                k.stt(h, psa, vcol(k, "b_glu", layer, cc), s, ALU.add, ALU.mult, [ka, sk, k.vk], [hk])
                k.dma("sp", h_d[cc * 128:(cc + 1) * 128, tsl], h, [hk], ["h_d"], f"hst{n%2}")
                k.dma("sp", halo_own[cc * 128:(cc + 1) * 128, t * 128:(t + 1) * 128].rearrange("p (b f) -> p b f", b=4),
                      h.rearrange("p (b f) -> p b f", b=4)[:, :, 96:128], [hk], ["halo_own"], f"hhalo{n%2}")
                n += 1


def phase_conv(k, layer, h_d, halo_all, yc_d, mixT_d):
    P = k.P
    k.phase()
    hb = [k.alloc(NBLK * 160, BF16) for _ in range(3)]
    g0 = [k.alloc(256, F32) for _ in range(3)]
    g1 = [k.alloc(256, F32) for _ in range(3)]
    dg = [k.alloc(CK * 128, BF16) for _ in range(3)]
    ycres, _ = k.alloc(8 * T, F32)
    ycres = ycres.rearrange("p (c t) -> p c t", c=8)
    yks = [f"ycres{c}" for c in range(8)]
    scr = ln_scratch(k)
    o16g = [k.alloc(4 * 512, BF16) for _ in range(3)]
    s_a = k.vecs[:, VG["s_a"]:VG["s_a"] + 1]
    s_b = k.vecs[:, VG["s_b"]:VG["s_b"] + 1]
    def prep(cc):
        h, hk = hb[cc % 3]
        h3 = h.rearrange("p (b f) -> p b f", b=NBLK)
        a0, a0k = g0[cc % 3]
        a1, a1k = g1[cc % 3]
        d, dk = dg[cc % 3]
        rows = slice(cc * 128, (cc + 1) * 128)
        k.dma("pool", h3[:, :, 32:160], h_d[rows, :].rearrange("p (b f) -> p b f", b=NBLK), ["h_d"], [hk + "m"], f"cvh{cc%3}")
        k.dma("sp", a0, halo_all[rows, :], ["halo_all"], [a0k], f"cvg0{cc%3}")
        k.dma("sp", a1, halo_all[1024 + cc * 128:1024 + (cc + 1) * 128, :], ["halo_all"], [a1k], f"cvg1{cc%3}")
        P.op("dve", lambda e, h3=h3: e.memset(h3[:, 0:1, 0:32], 0.0), writes=[hk + "h"])
        k.ts("dve", h3[:, 1:8, 0:32], a1.rearrange("p (b f) -> p b f", b=NBLK)[:, 0:7, :], s_a, None, ALU.mult, None,
             [a1k, k.vk], [hk + "h"])
        k.stt(h3[:, :, 0:32], a0.rearrange("p (b f) -> p b f", b=NBLK), s_b, h3[:, :, 0:32], ALU.mult, ALU.add,
              [a0k, k.vk, hk + "h"], [hk + "h"])
        for tap in range(CK):
            w = vcol(k, "w_dw", layer, cc * CK + tap)
            dst = d[:, tap * 128:(tap + 1) * 128]
            if tap % 2 == 0:
                k.ts("dve", dst, k.ident_f, w, None, ALU.mult, None, [k.ck, k.vk], [dk + f"_{tap}"])
            else:
                k.act(dst, k.ident_f, AF.Copy, [k.ck, k.vk], [dk + f"_{tap}"], scale=w)

    n = 0
    prep(0)
    for cc in range(8):
        if cc + 1 < 8:
            prep(cc + 1)
        h, hk = hb[cc % 3]
        h3 = h.rearrange("p (b f) -> p b f", b=NBLK)
        d, dk = dg[cc % 3]
        rows = slice(cc * 128, (cc + 1) * 128)
        dkeys = [dk + f"_{tap}" for tap in range(CK)]
        for half in range(2):
            ps, pk = k.bank()
            ps3 = ps.rearrange("p (b f) -> p b f", b=4)
            fns = [lambda e, tap=tap, ps3=ps3, h3=h3, d=d, half=half: e.matmul(
                ps3, lhsT=d[:, tap * 128:(tap + 1) * 128], rhs=h3[:, half * 4:(half + 1) * 4, 2 + tap:130 + tap],
                start=(tap == 0), stop=(tap == CK - 1)) for tap in range(CK)]
            P.op("pe", fns, reads=dkeys + [hk + "m", hk + "h"], writes=[pk])
            k.act(ycres[:, cc, half * 512:(half + 1) * 512], ps, AF.Identity, [pk, k.vk], [yks[cc]],
                  bias=vcol(k, "b_dw", layer, cc))
            n += 1
    def chunk(c, t):
        return ycres[:, c, t * 512:(t + 1) * 512], yks[c]

    stats = ln_stats(k, chunk, 8, LN_EPS, scr)
    cnt = {"n": 0}

    def epi(t, c0, grp):
        tsl = slice(t * 512, (t + 1) * 512)
        ob, obk = o16g[cnt["n"] % 3]
        for i, (x, xk) in enumerate(grp):
            k.act(ob[:, i * 512:(i + 1) * 512], x, AF.Silu, [xk, k.vk], [obk], scale=vcol(k, "g_cln", layer, c0 + i),
                  bias=vcol(k, "b_cln", layer, c0 + i))
        k.dma("sp", mixT_d[1024 + c0 * 128:1024 + (c0 + 4) * 128, tsl].rearrange("(c p) n -> p c n", p=128),
              ob.rearrange("p (c n) -> p c n", c=4), [obk], ["mixT_d"], f"lnst{cnt['n']%4}")
        cnt["n"] += 1

    ln_norm(k, chunk, stats, 8, epi)


def phase_attn(k, layer, cqn_d, lat_all, pos_d, mask_d, w_uq, w_ukv, mixT_d):
    P = k.P
    k.phase()
    rope_tables(k, pos_d)
    mf, mfk = k.alloc(NBLK * 256, F32)
    k.dma("sp", mf, mask_d, [], [mfk], "maskf")
    mb, mbk = k.alloc(NBLK * 256, BF16)
    P.op("dve", lambda e: e.tensor_copy(out=mb, in_=mf), reads=[mfk], writes=[mbk])
    mb3 = mb.rearrange("p (j c) -> p j c", j=NBLK)
    cqn, cqk = k.alloc(4 * T, BF16)
    cqn = cqn.rearrange("p (c t) -> p c t", c=4)
    k.dma("sp", cqn, cqn_d.rearrange("(c p) t -> p c t", p=128), ["cqn_d"], [cqk], "cqn")
    lat, _ = k.alloc(2 * 2048, BF16)
    lat = lat.rearrange("p (c t) -> p c t", c=2)
    kra, _ = k.alloc(2048, BF16, parts=64)
    lks = []
    for r in range(2):
        for kc in range(2):
            kk = f"lat{r}{kc}"
            k.dma("sp", lat[:, kc, r * 1024:(r + 1) * 1024], lat_all[r * 320 + kc * 128:r * 320 + (kc + 1) * 128, :],
                  ["lat_all"], [kk], kk)
            lks.append(kk)
        kk = f"kra{r}"
        k.dma("sp", kra[:, r * 1024:(r + 1) * 1024], lat_all[r * 320 + 256:r * 320 + 320, :], ["lat_all"], [kk], kk)
        lks.append(kk)
    wq = [k.alloc(4 * 192, BF16) for _ in range(2)]
    wkv = [k.alloc(2 * 256, BF16) for _ in range(2)]
    qn = [k.alloc(T, BF16) for _ in range(2)]
    qr = [k.alloc(T, BF16, parts=64) for _ in range(2)]
    qrf = [k.alloc(T, F32, parts=64) for _ in range(2)]
    kT = [k.alloc(2048, BF16) for _ in range(2)]
    Vh = [k.alloc(16 * 129, BF16) for _ in range(2)]
    mixh = [k.alloc(T, BF16) for _ in range(2)]
    Pb = [k.alloc(2048, BF16) for _ in range(2)]
    PT = [k.alloc(2048, BF16) for _ in range(2)]
    att = [k.alloc(128, BF16) for _ in range(2)]
    mx = [k.alloc(1, F32) for _ in range(2)]
    rs = [k.alloc(1, F32) for _ in range(2)]
    for i in range(2):
        v3 = Vh[i][0].rearrange("p (b f) -> p b f", b=16)
        P.op("pool", lambda e, v3=v3: e.memset(v3[:, :, 128:129], 1.0), writes=[Vh[i][1] + "o"])
    qscr = rope_scratch(k, T)
    psT = [(k.ps[:, 4 * 512:5 * 512].bitcast(BF16), "ps4"), (k.ps[:, 5 * 512:6 * 512].bitcast(BF16), "ps5")]
    psO, kO = k.ps[:, 6 * 512:7 * 512], "ps6"
    psM, kM = k.ps[:, 7 * 512:8 * 512], "ps7"
    psMb = psM.bitcast(BF16)
    state = {"pt": 0}

    def head_setup(h):
        b = h % 2
        wqa, wqk = wq[b]
        wq3 = wqa.rearrange("p (c n) -> p c n", c=4)
        k.dma("pool", wq3, w_uq[:, h * 192:(h + 1) * 192].rearrange("(c p) n -> p c n", p=128), [], [wqk], f"wq{b}")
        wka, wkk = wkv[b]
        wk3 = wka.rearrange("p (c n) -> p c n", c=2)
        k.dma("pool", wk3, w_ukv[:, h * 256:(h + 1) * 256].rearrange("(c p) n -> p c n", p=128), [], [wkk], f"wkv{b}")
        qna, qnk = qn[b]
        qfa, qfk = qrf[b]
        for t in range(2):
            tsl = slice(t * 512, (t + 1) * 512)
            k.mm(psM, [(wq3[:, kc, 0:128], cqn[:, kc, tsl]) for kc in range(4)], [wqk, cqk], [kM])
            k.act(qna[:, tsl], psM, AF.Copy, [kM], [qnk], scale=SCALE)
            k.mm(psM[0:64, :], [(wq3[:, kc, 128:192], cqn[:, kc, tsl]) for kc in range(4)], [wqk, cqk], [kM])
            k.act(qfa[:, tsl], psM[0:64, :], AF.Copy, [kM], [qfk])
        rope_apply(k, qfa, qfk, qr[b][0], qr[b][1], slice(0, T), SCALE, "q", qscr)
        kTa, kTk = kT[b]
        for t in range(4):
            tsl = slice(t * 512, (t + 1) * 512)
            k.mm(psM, [(wk3[:, kc, 0:128], lat[:, kc, tsl]) for kc in range(2)], [wkk] + lks, [kM])
            k.act(kTa[:, tsl], psM, AF.Copy, [kM], [kTk])
        va, vk = Vh[b]
        v3 = va.rearrange("p (b f) -> p b f", b=16)
        for g in range(4):
            fns = []
            for i in range(4):
                kb = g * 4 + i
                for kc in range(2):
                    fns.append(lambda e, i=i, kb=kb, kc=kc: e.matmul(psM[:, i * 128:(i + 1) * 128],
                                                                     lhsT=lat[:, kc, kb * 128:(kb + 1) * 128],
                                                                     rhs=wk3[:, kc, 128:256], start=(kc == 0), stop=(kc == 1)))
            P.op("pe", fns, reads=[wkk] + lks, writes=[kM])
            P.op("dve", lambda e, g=g, v3=v3: e.tensor_copy(out=v3[:, g * 4:(g + 1) * 4, 0:128],
                                                           in_=psM.rearrange("p (b f) -> p b f", b=4)),
                 reads=[kM], writes=[vk])

    def score_banks(j):
        ncol = 2 * (j + 1) * 128
        if ncol <= 1024:
            return (j % 2) * 2
        return 0

    def qk(h, j):
        b = h % 2
        nA = (j + 1) * 128
        ncol = 2 * nA
        b0 = score_banks(j)
        fns = []
        wkeys = []
        nseg = (ncol + 511) // 512
        for s in range(nseg):
            lo, hi = s * 512, min((s + 1) * 512, ncol)
            bankoff = (b0 + s) * 512
            wkeys.append(f"ps{b0 + s}")
            for (plo, phi, kbase) in ((0, nA, 0), (nA, ncol, 1024 - nA)):
                a, z = max(lo, plo), min(hi, phi)
                if a >= z:
                    continue
                out = k.ps[:, bankoff + (a - lo):bankoff + (z - lo)]
                kcols = slice(kbase + a, kbase + z)
                fns.append(lambda e, out=out, kcols=kcols: e.matmul(out, lhsT=qn[b][0][:, j * 128:(j + 1) * 128],
                                                                    rhs=kT[b][0][:, kcols], start=True, stop=False))
                mlo = phi - 128
                has_mask = (a <= mlo and z >= phi)
                fns.append(lambda e, out=out, kcols=kcols, hm=has_mask: e.matmul(
                    out, lhsT=qr[b][0][:, j * 128:(j + 1) * 128], rhs=kra[:, kcols], start=False, stop=(not hm)))
                if has_mask:
                    mout = k.ps[:, bankoff + (mlo - lo):bankoff + (phi - lo)]
                    mcol = 0 if plo == 0 else 128
                    fns.append(lambda e, mout=mout, mcol=mcol: e.matmul(mout, lhsT=k.ident_b,
                                                                        rhs=mb3[:, j, mcol:mcol + 128],
                                                                        start=False, stop=True))
        P.op("pe", fns, reads=[qn[b][1], qr[b][1], kT[b][1], mbk, k.ibk] + lks, writes=wkeys)
        return b0, ncol, wkeys

    def softmax(h, j, b0, ncol, wkeys, n):
        S = k.ps[:, b0 * 512:b0 * 512 + ncol]
        m, mk = mx[n % 2]
        P.op("dve", lambda e: e.tensor_reduce(out=m, in_=S, axis=AX.X, op=ALU.max, negate=True), reads=wkeys, writes=[mk])
        pb, pbk = Pb[n % 2]
        k.act(pb[:, 0:ncol], S, AF.Exp, wkeys + [mk], [pbk], bias=m)

    def tpv(h, j, ncol, n):
        b = h % 2
        nk = ncol // 128
        nA = (j + 1)
        pb, pbk = Pb[n % 2]
        pt, ptk = PT[n % 2]
        for r0 in range(0, nk, 8):
            r1 = min(r0 + 8, nk)
            tb, tbk = psT[state["pt"] % 2]
            state["pt"] += 1
            fns = [lambda e, i=i, tb=tb, r0=r0: e.transpose(tb[:, (i - r0) * 128:(i - r0 + 1) * 128], pb[:, i * 128:(i + 1) * 128],
                                                     k.ident_b) for i in range(r0, r1)]
            P.op("pe", fns, reads=[pbk, k.ibk], writes=[tbk])
            k.act(pt[:, r0 * 128:r1 * 128], tb[:, 0:(r1 - r0) * 128], AF.Copy, [tbk], [ptk + f"r{r0}"])

    def pv(h, j, ncol, n):
        b = h % 2
        nk = ncol // 128
        nA = (j + 1)
        pt, ptk = PT[n % 2]
        va, vk = Vh[b]
        v3 = va.rearrange("p (b f) -> p b f", b=16)
        fns = []
        for i in range(nk):
            kb = i if i < nA else 8 + (i - nA)
            fns.append(lambda e, i=i, kb=kb: e.matmul(psO[:, 0:129], lhsT=pt[:, i * 128:(i + 1) * 128], rhs=v3[:, kb, :],
                                                      start=(i == 0), stop=(i == nk - 1)))
        P.op("pe", fns, reads=[ptk + "r0", ptk + "r8", vk, vk + "o"], writes=[kO])
        r, rk = rs[n % 2]
        P.op("dve", lambda e: e.reciprocal(out=r, in_=psO[:, 128:129]), reads=[kO], writes=[rk])
        a, ak = att[n % 2]
        k.ts("dve", a, psO[:, 0:128], r, None, ALU.mult, None, [kO, rk], [ak])

    def tf(h, j, ncol, n):
        b = h % 2
        a, ak = att[n % 2]
        tb, tbk = psMb, kM
        P.op("pe", lambda e: e.transpose(tb[:, 0:128], a, k.ident_b), reads=[ak, k.ibk], writes=[tbk])
        ma, mk_ = mixh[b]
        k.act(ma[:, j * 128:(j + 1) * 128], tb[:, 0:128], AF.Copy, [tbk], [mk_])
        if j == NBLK - 1:
            k.dma("sp", mixT_d[h * 128:(h + 1) * 128, :], ma, [mk_], ["mixT_d"], f"mixst{b}")

    steps = [(h, j) for h in range(NH) for j in range(NBLK)]
    NS = len(steps)
    info = {}
    head_setup(0)
    info[0] = qk(*steps[0])
    for n in range(NS + 2):
        if n < NS:
            h, j = steps[n]
            b0, ncol, wkeys = info[n]
            softmax(h, j, b0, ncol, wkeys, n)
            if j == 2 and h + 1 < NH:
                head_setup(h + 1)
        if n + 1 < NS:
            info[n + 1] = qk(*steps[n + 1])
        if n < NS:
            tpv(*steps[n], info[n][1], n)
        if 0 <= n - 1 < NS:
            pv(*steps[n - 1], info[n - 1][1], n - 1)
        if 0 <= n - 2 < NS:
            tf(*steps[n - 2], info[n - 2][1], n - 2)


def phase_attn2(k, layer, cqn_d, lat_all, pos_d, mask_d, w_uq, w_ukv, mixT_d):
    P = k.P
    k.phase()
    rope_tables(k, pos_d)
    mb, mbk = k.alloc(NBLK * 256, BF16)
    k.dma("pool", mb, mask_d, [], [mbk], "maskb")
    mb3 = mb.rearrange("p (j c) -> p j c", j=NBLK)
    ones_b, obk = k.alloc(128, BF16, parts=1)
    P.op("dve", lambda e: e.tensor_copy(out=ones_b, in_=k.ones_f[0:1, :]), reads=[k.ck], writes=[obk])
    cqn, cqk = k.alloc(4 * T, BF16)
    cqn = cqn.rearrange("p (c t) -> p c t", c=4)
    k.dma("sp", cqn, cqn_d.rearrange("(c p) t -> p c t", p=128), ["cqn_d"], [cqk], "cqn")
    lat, _ = k.alloc(2 * 2048, BF16)
    lat = lat.rearrange("p (c t) -> p c t", c=2)
    kra, _ = k.alloc(2048, BF16, parts=65)
    P.op("pool", lambda e: e.memset(kra[64:65, :], 1.0), writes=["kra_ones"])
    onesb_col, ocbk = k.alloc(1, BF16)
    P.op("dve", lambda e: e.tensor_copy(out=onesb_col, in_=k.ones_f[:, 0:1]), reads=[k.ck], writes=[ocbk])
    lks = []
    for r in range(2):
        for kc in range(2):
            kk = f"lat{r}{kc}"
            k.dma("sp", lat[:, kc, r * 1024:(r + 1) * 1024], lat_all[r * 320 + kc * 128:r * 320 + (kc + 1) * 128, :],
                  ["lat_all"], [kk], kk)
            lks.append(kk)
        kk = f"kra{r}"
        k.dma("sp", kra[0:64, r * 1024:(r + 1) * 1024], lat_all[r * 320 + 256:r * 320 + 320, :], ["lat_all"], [kk], kk)
        lks.append(kk)
    wq = [k.alloc(4 * 192, BF16) for _ in range(2)]
    wkv = [k.alloc(2 * 256, BF16) for _ in range(2)]
    qn = [k.alloc(T, BF16) for _ in range(2)]
    qr = [k.alloc(T, BF16, parts=65) for _ in range(2)]
    qrf = [k.alloc(T, F32, parts=64)] * 2
    kT = [k.alloc(2048, BF16) for _ in range(2)]
    Vh = [k.alloc(16 * 129, BF16) for _ in range(2)]
    mixh = [k.alloc(T, BF16) for _ in range(2)]
    NPT = 2 * 36 * 128
    PT = [k.alloc(NPT, BF16) for _ in range(2)]
    negm = [k.alloc(T, BF16, parts=1) for _ in range(2)]
    att = [k.alloc(128, BF16) for _ in range(2)]
    rs = [k.alloc(1, F32) for _ in range(2)]
    sqt = [k.alloc(512, BF16) for _ in range(3)]
    q2row, q2k = k.alloc(T, F32, parts=1)
    k2row, k2k = k.alloc(512, F32, parts=1)
    kmax4, km4k = k.alloc(4, F32, parts=1)
    kr2row, kr2k = k.alloc(2048, F32, parts=1)
    kmax, kmk = k.alloc(1, F32, parts=1)
    for i in range(2):
        v3 = Vh[i][0].rearrange("p (b f) -> p b f", b=16)
        P.op("pool", lambda e, v3=v3: e.memset(v3[:, :, 128:129], 1.0), writes=[Vh[i][1] + "o"])
    qscr = rope_scratch(k, T)
    psR, kR = k.ps[:, 5 * 512:6 * 512], "ps5"
    psO, kO = k.ps[:, 6 * 512:7 * 512], "ps6"
    psM, kM = k.ps[:, 7 * 512:8 * 512], "ps7"
    psMb = psM.bitcast(BF16)
    sbanks = [(k.ps[:, 4 * 512:5 * 512], "ps4"), (psM, kM)]
    ones_col = k.ones_f[:, 0:1]
    st = {"bank": 0, "sq": 0, "sb": 0}

    def sbank():
        x_ = sbanks[st["sb"] % 2]
        st["sb"] += 1
        return x_
    bases = {}
    off = 0
    for r in range(2):
        for p in range(NBLK):
            bases[r * 8 + p] = off
            off += (NBLK - p) * 128

    def sq_next():
        a = sqt[st["sq"] % 3]
        st["sq"] += 1
        return a

    for t in range(4):
        tsl = slice(t * 512, (t + 1) * 512)
        s_, sk = sq_next()
        k.act(s_[0:64, :], kra[0:64, tsl], AF.Square, lks, [sk])
        k.mm(psR[0:1, :], [(onesb_col[0:64, :], s_[0:64, :])], [ocbk, sk], [kR])
        k.act(kr2row[:, tsl], psR[0:1, :], AF.Copy, [kR], [kr2k])

    def head_setup(h):
        b = h % 2
        pend = []

        def flush():
            while pend:
                pend.pop(0)()
        wqa, wqk = wq[b]
        wq3 = wqa.rearrange("p (c n) -> p c n", c=4)
        k.dma("pool", wq3, w_uq[:, h * 192:(h + 1) * 192].rearrange("(c p) n -> p c n", p=128), [], [wqk], f"wq{b}")
        wka, wkk = wkv[b]
        wk3 = wka.rearrange("p (c n) -> p c n", c=2)
        k.dma("pool", wk3, w_ukv[:, h * 256:(h + 1) * 256].rearrange("(c p) n -> p c n", p=128), [], [wkk], f"wkv{b}")
        yield
        qna, qnk = qn[b]
        qfa, qfk = qrf[b]
        for t in range(2):
            tsl = slice(t * 512, (t + 1) * 512)
            flush()
            pm, pmk = sbank()
            k.mm(pm, [(wq3[:, kc, 0:128], cqn[:, kc, tsl]) for kc in range(4)], [wqk, cqk], [pmk])
            k.act(qna[:, tsl], pm, AF.Copy, [pmk], [qnk], scale=SCALE)
            s1, s1k = sq_next()
            k.act(s1, pm, AF.Square, [pmk], [s1k], scale=SCALE)
            pm2, pm2k = sbank()
            k.mm(pm2[0:64, :], [(wq3[:, kc, 128:192], cqn[:, kc, tsl]) for kc in range(4)], [wqk, cqk], [pm2k])
            k.act(qfa[:, tsl], pm2[0:64, :], AF.Copy, [pm2k], [qfk])
            s2, s2k = sq_next()
            k.act(s2[0:64, :], pm2[0:64, :], AF.Square, [pm2k], [s2k], scale=SCALE)

            def stat_q(s1=s1, s1k=s1k, s2=s2, s2k=s2k, tsl=tsl):
                k.mm(psR[0:1, :], [(onesb_col, s1), (onesb_col[0:64, :], s2[0:64, :])], [ocbk, s1k, s2k], [kR])
                k.act(q2row[:, tsl], psR[0:1, :], AF.Copy, [kR], [q2k])
            pend.append(stat_q)
            yield
        rope_apply(k, qfa, qfk, qr[b][0][0:64, :], qr[b][1], slice(0, T), SCALE, "q", qscr)
        yield
        kTa, kTk = kT[b]
        for t in range(4):
            tsl = slice(t * 512, (t + 1) * 512)
            flush()
            pm, pmk = sbank()
            k.mm(pm, [(wk3[:, kc, 0:128], lat[:, kc, tsl]) for kc in range(2)], [wkk] + lks, [pmk])
            k.act(kTa[:, tsl], pm, AF.Copy, [pmk], [kTk])
            s1, s1k = sq_next()
            k.act(s1, pm, AF.Square, [pmk], [s1k])

            def stat_k(s1=s1, s1k=s1k, tsl=tsl, t=t):
                k.mm(psR[0:1, :], [(onesb_col, s1)], [ocbk, s1k], [kR])
                k.tt("dve", k2row, psR[0:1, :], kr2row[:, tsl], ALU.add, [kR, kr2k], [k2k])
                P.op("dve", lambda e, t=t: e.tensor_reduce(out=kmax4[:, t:t + 1], in_=k2row, axis=AX.X, op=ALU.max),
                     reads=[k2k], writes=[km4k])
            pend.append(stat_k)
            yield
        yield
        flush()
        P.op("dve", lambda e: e.tensor_reduce(out=kmax, in_=kmax4, axis=AX.X, op=ALU.max), reads=[km4k], writes=[kmk])
        nm, nmk = negm[b]
        k.ts("dve", q2row, q2row, kmax, None, ALU.mult, None, [q2k, kmk], [q2k])
        k.act(q2row, q2row, AF.Sqrt, [q2k], [q2k])
        k.act(nm, q2row, AF.Copy, [q2k], [nmk], scale=-1.0)
        k.dma("sp", qr[b][0][64:65, :], nm, [nmk], [qr[b][1] + "m"], f"nmrow{b}")
        yield
        va, vk = Vh[b]
        v3 = va.rearrange("p (b f) -> p b f", b=16)
        for g in range(4):
            pm, pmk = sbank()
            fns = []
            for i in range(4):
                kb = g * 4 + i
                for kc in range(2):
                    fns.append(lambda e, i=i, kb=kb, kc=kc, pm=pm: e.matmul(pm[:, i * 128:(i + 1) * 128],
                                                                            lhsT=lat[:, kc, kb * 128:(kb + 1) * 128],
                                                                            rhs=wk3[:, kc, 128:256], start=(kc == 0), stop=(kc == 1)))
            P.op("pe", fns, reads=[wkk] + lks, writes=[pmk])
            P.op("dve", lambda e, g=g, v3=v3, pm=pm: e.tensor_copy(out=v3[:, g * 4:(g + 1) * 4, 0:128],
                                                                  in_=pm.rearrange("p (b f) -> p b f", b=4)),
                 reads=[pmk], writes=[vk])
            if g % 2 == 1:
                yield

    def qkt(h, p):
        b = h % 2
        pt, ptk = PT[b]
        q0 = p * 128
        for r in range(2):
            i = r * 8 + p
            kcols = slice(r * 1024 + p * 128, r * 1024 + (p + 1) * 128)
            for c0 in range(q0, T, 512):
                c1 = min(c0 + 512, T)
                n = c1 - c0
                bk = st["bank"] % 4
                st["bank"] += 1
                ps, pk = k.ps[:, bk * 512:bk * 512 + n], f"ps{bk}"
                diag = (c0 == q0)
                fns = [lambda e, ps=ps, kcols=kcols, c0=c0, c1=c1: e.matmul(ps, lhsT=kT[b][0][:, kcols], rhs=qn[b][0][:, c0:c1],
                                                                           start=True, stop=False),
                       lambda e, ps=ps, kcols=kcols, c0=c0, c1=c1, diag=diag: e.matmul(
                           ps, lhsT=kra[0:65, kcols], rhs=qr[b][0][0:65, c0:c1], start=False, stop=(not diag))]
                if diag:
                    fns.append(lambda e, ps=ps, r=r: e.matmul(ps[:, 0:128], lhsT=mb3[:, p, r * 128:(r + 1) * 128], rhs=k.ident_b,
                                                              start=False, stop=True))
                P.op("pe", fns, reads=[kT[b][1], qn[b][1], qr[b][1], qr[b][1] + "m", "kra_ones", mbk, k.ibk] + lks, writes=[pk])
                o0 = bases[i] + (c0 - q0)
                k.act(pt[:, o0:o0 + n], ps, AF.Exp, [pk], [ptk + f"_{i}"])

    def pvj(h, j):
        b = h % 2
        pt, ptk = PT[b]
        va, vk = Vh[b]
        v3 = va.rearrange("p (b f) -> p b f", b=16)
        blocks = [r * 8 + p for r in range(2) for p in range(j + 1)]
        fns = []
        for x, i in enumerate(blocks):
            p = i % 8
            o0 = bases[i] + (j - p) * 128
            fns.append(lambda e, x=x, i=i, o0=o0: e.matmul(psO[:, 0:129], lhsT=pt[:, o0:o0 + 128], rhs=v3[:, i, :],
                                                           start=(x == 0), stop=(x == len(blocks) - 1)))
        P.op("pe", fns, reads=[ptk + f"_{i}" for i in blocks] + [vk, vk + "o"], writes=[kO])
        n = h * NBLK + j
        r_, rk = rs[n % 2]
        P.op("dve", lambda e: e.reciprocal(out=r_, in_=psO[:, 128:129]), reads=[kO], writes=[rk])
        a, ak = att[n % 2]
        k.ts("dve", a, psO[:, 0:128], r_, None, ALU.mult, None, [kO, rk], [ak])

    def tf(h, j):
        b = h % 2
        n = h * NBLK + j
        a, ak = att[n % 2]
        P.op("pe", lambda e: e.transpose(psMb[:, 0:128], a, k.ident_b), reads=[ak, k.ibk], writes=[kM])
        ma, mk_ = mixh[b]
        k.act(ma[:, j * 128:(j + 1) * 128], psMb[:, 0:128], AF.Copy, [kM], [mk_])
        if j == NBLK - 1:
            k.dma("sp", mixT_d[h * 128:(h + 1) * 128, :], ma, [mk_], ["mixT_d"], f"mixst{b}")

    steps = [(h, p) for h in range(NH) for p in range(NBLK)]
    NS = len(steps)
    for _ in head_setup(0):
        pass
    gen = None
    qkt(*steps[0])
    for n in range(NS + 1):
        if n < NS:
            h, p = steps[n]
            if p == 0 and h + 1 < NH:
                gen = head_setup(h + 1)
            if gen is not None:
                for _ in range(2):
                    try:
                        next(gen)
                    except StopIteration:
                        gen = None
                        break
            if p == NBLK - 2 and gen is not None:
                for _ in gen:
                    pass
                gen = None
        if n + 1 < NS:
            qkt(*steps[n + 1])
        if n < NS:
            pvj(*steps[n])
        if 0 <= n - 1 < NS:
            tf(*steps[n - 1])


def phase_outproj(k, layer, mixT_d, XF_d, xfk, w_out, y_d, XF_o, xfok, XT_o, xtok):
    P = k.P
    k.phase()
    mt, _ = k.alloc(16 * T, BF16)
    mt = mt.rearrange("p (c t) -> p c t", c=16)
    mks = []
    for q in range(4):
        kk = f"xtq{q}"
        k.dma("sp", mt[:, q * 4:(q + 1) * 4, :], mixT_d[q * 512:(q + 1) * 512, :].rearrange("(c p) t -> p c t", p=128),
              ["mixT_d"], [kk], kk)
        mks.append(kk)
    k.ring_init(4, 8192)
    xf = [k.alloc(512, F32) for _ in range(3)]
    yres, _ = k.alloc(16 * T, F32)
    yres = yres.rearrange("p (c t) -> p c t", c=16)
    yks = [f"yres{c}" for c in range(16)]
    n = 0
    for pn in range(4):
        pan, pkey = k.wload(w_out[:, pn * 512:(pn + 1) * 512], 16, 512)
        for o4 in range(4):
            oc = pn * 4 + o4
            for t in range(2):
                tsl = slice(t * 512, (t + 1) * 512)
                ps, pk = k.bank()
                k.mm(ps, [(pan[:, kc, o4 * 128:(o4 + 1) * 128], mt[:, kc, tsl]) for kc in range(16)], [pkey] + mks, [pk])
                x, xk = xf[n % 3]
                k.dma("sp", x, XF_d[oc * 128:(oc + 1) * 128, tsl], [xfk], [xk], f"opx{n%3}")
                k.stt(yres[:, oc, tsl], x, ALPHA, ps, ALU.mult, ALU.add, [xk, pk], [yks[oc]])
                n += 1
    lnfm(k, None, None, 16, "ln1_g", "ln1_b", layer, LN_EPS, AF.Identity, dst_f=XF_o, dfk=xfok, dst_b=XT_o, dbk=xtok,
         res=(yres, yks))


def phase_ffn(k, layer, XF_d, xfk, XT_d, xtk, w1, w2, y_d, XF_o, xfok, XT_o, xtok):
    P = k.P
    k.phase()
    acc, _ = k.alloc(16 * T, F32)
    acc = acc.rearrange("p (c t) -> p c t", c=16)
    mark = k.off
    xt, _ = k.alloc(16 * T, BF16)
    xt = xt.rearrange("p (c t) -> p c t", c=16)
    xks = []
    for q in range(4):
        kk = f"xtq{q}"
        k.dma("sp", xt[:, q * 4:(q + 1) * 4, :], XT_d[q * 512:(q + 1) * 512, :].rearrange("(c p) t -> p c t", p=128),
              [xtk], [kk], kk)
        xks.append(kk)
    aks = []
    for c in range(16):
        kk = f"acc{c}"
        k.dma("sp", acc[:, c, :], XF_d[c * 128:(c + 1) * 128, :], [xfk], [kk], kk)
        k.act(acc[:, c, :], acc[:, c, :], AF.Copy, [kk], [kk], scale=ALPHA)
        aks.append(kk)
    k.ring_init(4, 8192)
    h1 = [k.alloc(4 * T, BF16) for _ in range(2)]
    rt = [k.alloc(512, F32) for _ in range(2)]
    n = 0
    NG = DFF // 512
    for g in range(NG):
        p1, p1k = k.wload(w1[:, g * 512:(g + 1) * 512], 16, 512)
        p2, p2k = k.wload(w2[g * 512:(g + 1) * 512, :], 4, 2048)
        ha, hk = h1[g % 2]
        h3 = ha.rearrange("p (c t) -> p c t", c=4)
        for fc in range(4):
            for t in range(2):
                tsl = slice(t * 512, (t + 1) * 512)
                ps, pk = k.bank()
                k.mm(ps, [(p1[:, kc, fc * 128:(fc + 1) * 128], xt[:, kc, tsl]) for kc in range(16)], [p1k] + xks, [pk])
                r, rk = rt[n % 2]
                k.act(r, ps, AF.Relu, [pk], [rk])
                k.act(h3[:, fc, tsl], r, AF.Square, [rk], [hk + f"_{fc}_{t}"])
                n += 1
        hkeys = [hk + f"_{fc}_{t}" for fc in range(4) for t in range(2)]
        for oc in range(16):
            for t in range(2):
                tsl = slice(t * 512, (t + 1) * 512)
                ps, pk = k.bank()
                k.mm(ps, [(p2[:, fc, oc * 128:(oc + 1) * 128], h3[:, fc, tsl]) for fc in range(4)], [p2k] + hkeys, [pk])
                k.tt("dve", acc[:, oc, tsl], acc[:, oc, tsl], ps, ALU.add, [aks[oc], pk], [aks[oc]])
    k.P.barrier()
    k.off = mark
    lnfm(k, None, None, 16, "ln2_g", "ln2_b", layer, LN_EPS, AF.Identity, dst_f=XF_o, dfk=xfok, dst_b=XT_o, dbk=xtok,
         res=(acc, aks))


def phase_out_ffn(k, layer, mixT_d, XF_d, xfk, w_out, w1, w2, XF_o, xfok):
    P = k.P
    k.phase()
    X, _ = k.alloc(16 * T, BF16)
    X = X.rearrange("p (c t) -> p c t", c=16)
    acc, _ = k.alloc(16 * T, F32)
    acc = acc.rearrange("p (c t) -> p c t", c=16)
    aks = [f"acc{c}" for c in range(16)]
    scr = ln_scratch(k)
    gbs, gbk = k.alloc(32, F32)
    k.ts("dve", gbs, k.vecs[:, vcol_idx(k, "ln1_g", layer):vcol_idx(k, "ln1_g", layer) + 32], ALPHA, None, ALU.mult, None,
         [k.vk], [gbk])
    mark = k.off
    mks = []
    for q in range(4):
        kk = f"xtq{q}"
        k.dma("sp", X[:, q * 4:(q + 1) * 4, :], mixT_d[q * 512:(q + 1) * 512, :].rearrange("(c p) t -> p c t", p=128),
              ["mixT_d"], [kk], kk)
        mks.append(kk)
    xf = [k.alloc(512, F32) for _ in range(3)]
    n = 0
    for pn in range(4):
        pan, pkey = k.wload(w_out[:, pn * 512:(pn + 1) * 512], 16, 512)
        for o4 in range(4):
            oc = pn * 4 + o4
            for t in range(2):
                tsl = slice(t * 512, (t + 1) * 512)
                ps, pk = k.bank()
                k.mm(ps, [(pan[:, kc, o4 * 128:(o4 + 1) * 128], X[:, kc, tsl]) for kc in range(16)], [pkey] + mks, [pk])
                x, xk = xf[n % 3]
                k.dma("sp", x, XF_d[oc * 128:(oc + 1) * 128, tsl], [xfk], [xk], f"opx{n%3}")
                k.stt(acc[:, oc, tsl], x, ALPHA, ps, ALU.mult, ALU.add, [xk, pk], [aks[oc]])
                n += 1

    def chunk(c, t):
        return acc[:, c, t * 512:(t + 1) * 512], aks[c]

    stats = ln_stats(k, chunk, 16, LN_EPS, scr)
    P.barrier()
    k.off = mark
    xkeys = [f"X{c}" for c in range(16)]

    def epi1(t, c0, grp):
        tsl = slice(t * 512, (t + 1) * 512)
        for i, (x, xk) in enumerate(grp):
            c = c0 + i
            k.act(X[:, c, tsl], x, AF.Identity, [xk, k.vk], [xkeys[c]], scale=vcol(k, "ln1_g", layer, c),
                  bias=vcol(k, "ln1_b", layer, c))

            if c % 2 == 1:
                k.act(x, x, AF.Identity, [xk, gbk], [xk], scale=gbs[:, c:c + 1], bias=gbs[:, 16 + c:17 + c])

        def post(c0=c0, grp=grp):
            for i, (x, xk) in enumerate(grp):
                c = c0 + i
                if c % 2 == 0:
                    k.ts("dve", x, x, gbs[:, c:c + 1], gbs[:, 16 + c:17 + c], ALU.mult, ALU.add, [xk, gbk], [xk])
        return post

    ln_norm(k, chunk, stats, 16, epi1)
    h1 = [k.alloc(4 * T, BF16) for _ in range(2)]
    rt = [k.alloc(512, F32) for _ in range(2)]
    n = 0
    NG = DFF // 512
    for g in range(NG):
        p1, p1k = k.wload(w1[:, g * 512:(g + 1) * 512], 16, 512)
        p2, p2k = k.wload(w2[g * 512:(g + 1) * 512, :], 4, 2048)
        ha, hk = h1[g % 2]
        h3 = ha.rearrange("p (c t) -> p c t", c=4)
        for fc in range(4):
            for t in range(2):
                tsl = slice(t * 512, (t + 1) * 512)
                ps, pk = k.bank()
                k.mm(ps, [(p1[:, kc, fc * 128:(fc + 1) * 128], X[:, kc, tsl]) for kc in range(16)], [p1k] + xkeys, [pk])
                r, rk = rt[n % 2]
                k.act(r, ps, AF.Relu, [pk], [rk])
                k.act(h3[:, fc, tsl], r, AF.Square, [rk], [hk + f"_{fc}_{t}"])
                n += 1
        hkeys = [hk + f"_{fc}_{t}" for fc in range(4) for t in range(2)]
        for oc in range(16):
            for t in range(2):
                tsl = slice(t * 512, (t + 1) * 512)
                ps, pk = k.bank()
                k.mm(ps, [(p2[:, fc, oc * 128:(oc + 1) * 128], h3[:, fc, tsl]) for fc in range(4)], [p2k] + hkeys, [pk])
                k.tt("dve", acc[:, oc, tsl], acc[:, oc, tsl], ps, ALU.add, [aks[oc], pk], [aks[oc]])
    stats = ln_stats(k, chunk, 16, LN_EPS, scr)
    st = {"n": 0}

    def epi2(t, c0, grp):
        tsl = slice(t * 512, (t + 1) * 512)
        for i, (x, xk) in enumerate(grp):
            c = c0 + i
            k.act(X[:, c, tsl], x, AF.Identity, [xk, k.vk], [xkeys[c]], scale=vcol(k, "ln2_g", layer, c),
                  bias=vcol(k, "ln2_b", layer, c))

            if c % 2 == 1:
                k.act(x, x, AF.Identity, [xk, k.vk], [xk], scale=vcol(k, "ln2_g", layer, c), bias=vcol(k, "ln2_b", layer, c))

        def post(t=t, c0=c0, grp=grp, tsl=tsl):
            for i, (x, xk) in enumerate(grp):
                c = c0 + i
                if c % 2 == 0:
                    k.ts("dve", x, x, vcol(k, "ln2_g", layer, c), vcol(k, "ln2_b", layer, c), ALU.mult, ALU.add,
                         [xk, k.vk], [xk])
            k.dma("sp", XF_o[c0 * 128:(c0 + 4) * 128, tsl].rearrange("(c p) n -> p c n", p=128), acc[:, c0:c0 + 4, tsl],
                  [xk for (_, xk) in grp], [xfok], f"lnst{st['n']%4}")
            st["n"] += 1
        return post

    ln_norm(k, chunk, stats, 16, epi2)


def wsched_a(w_in):
    sc = [(w_in[:, 0:512], 16, 512), (w_in[:, 512:832], 16, 320)]
    for hp in range(2):
        sc.append((w_in[:, 832 + hp * 512:832 + (hp + 1) * 512], 16, 512))
        sc.append((w_in[:, 832 + 1024 + hp * 512:832 + 1024 + (hp + 1) * 512], 16, 512))
    return sc


def wsched_b(w_out, w1, w2):
    sc = [(w_out[:, pn * 512:(pn + 1) * 512], 16, 512) for pn in range(4)]
    for g in range(DFF // 512):
        sc.append((w1[:, g * 512:(g + 1) * 512], 16, 512))
        sc.append((w2[g * 512:(g + 1) * 512, :], 4, 2048))
    return sc


ATTN = phase_attn2
PAIRS = [[0, 1], [2, 3], [4, 5], [6, 7]]
DEBUG = False


def build(kind):
    nc = bass.Bass("TRN2", target_bir_lowering=False)

    def din(name, shape, dt=F32):
        return nc.dram_tensor(name, shape, dt, kind="ExternalInput").ap()

    def dout(name, shape, dt=F32):
        return nc.dram_tensor(name, shape, dt, kind="ExternalOutput").ap()

    def dint(name, shape, dt=F32):
        return nc.dram_tensor(name, shape, dt).ap()

    es = ExitStack()
    with es:
        k = K(nc, es, kind)
        nv = NV if kind == "FUSED" else NVG + NVL
        vecs_d = din("vecs", [128, nv])
        cst_d = din("cst", [128, 256])
        load_consts(k, vecs_d, cst_d, nv)
        outs = []
        if kind == "P0":
            xT = din("xT", [D, T])
            XF = dout("XF_o", [D, T])
            XT_ = dout("XT_o", [D, T], BF16)
            k.phase()
            lnfm(k, xT, "xT", 16, "ln_in_g", "ln_in_b", 0, LN_EPS, AF.Identity, dst_f=XF, dfk="XF_o", dst_b=XT_, dbk="XT_o")
            outs = ["XF_o", "XT_o"]
            k.ring_setup([])
        elif kind == "A":
            XT_ = din("XT_i", [D, T], BF16)
            pos = din("pos", [64, T], I32)
            w_in = din("w_in", [D, IN_COLS])
            cqn = dout("cqn_d", [QL, T], BF16)
            lat = dout("lat_own", [320, T], BF16)
            h_d = dout("h_d", [CW, T])
            halo = dout("halo_own", [CW, 256])
            k.ring_setup(wsched_a(w_in))
            k.base = k.off
            rope_tables(k, pos)
            phase_a(k, 0, XT_, "XT_i", pos, w_in, cqn, lat, h_d, halo)
            outs = ["cqn_d", "lat_own", "h_d", "halo_own"]
        elif kind == "B":
            XF = din("XF_i", [D, T])
            cqn = din("cqn_d", [QL, T], BF16)
            lat_all = din("lat_all", [640, T], BF16)
            h_d = din("h_d", [CW, T])
            halo_all = din("halo_all", [2 * CW, 256])
            pos = din("pos", [64, T], I32)
            maskf = din("maskf", [128, NBLK * 256])
            w_uq = din("w_uq", [QL, NH * 192])
            w_ukv = din("w_ukv", [KVL, NH * 256])
            w_out = din("w_out", [D, D])
            w1 = din("w1", [D, DFF])
            w2 = din("w2", [DFF, D])
            dbg = dout if DEBUG else dint
            yc_d = dbg("yc_d", [CW, T])
            mixT = dbg("mixT_d", [D, T], BF16)
            y_d = dint("y_d", [D, T])
            XF1 = dbg("XF_1", [D, T])
            XT1 = dint("XT_1", [D, T], BF16)
            XFo = dout("XF_o", [D, T])
            XTo = dout("XT_o", [D, T], BF16)
            k.ring_setup(wsched_b(w_out, w1, w2))
            k.base = k.off
            rope_tables(k, pos)
            phase_conv(k, 0, h_d, halo_all, yc_d, mixT)
            ATTN(k, 0, cqn, lat_all, pos, maskf, w_uq, w_ukv, mixT)
            phase_outproj(k, 0, mixT, XF, "XF_i", w_out, y_d, XF1, "XF_1", XT1, "XT_1")
            phase_ffn(k, 0, XF1, "XF_1", XT1, "XT_1", w1, w2, y_d, XFo, "XF_o", XTo, "XT_o")
            outs = ["XF_o", "XT_o"]
        elif kind == "FUSED":
            xT = din("xT", [D, T])
            pos = din("pos", [64, T], I32)
            maskf = din("maskf", [128, NBLK * 256])
            w_in = din("w_in", [DEPTH, D, IN_COLS])
            w_uq = din("w_uq", [DEPTH, QL, NH * 192])
            w_ukv = din("w_ukv", [DEPTH, KVL, NH * 256])
            w_out = din("w_out", [DEPTH, D, D])
            w1 = din("w1", [DEPTH, D, DFF])
            w2 = din("w2", [DEPTH, DFF, D])
            XF = [dint("XF_a", [D, T]), dint("XF_b", [D, T])]
            XT_ = [dint("XT_a", [D, T], BF16), dint("XT_b", [D, T], BF16)]
            cqn = dint("cqn_d", [QL, T], BF16)
            lat = dint("lat_own", [320, T], BF16)
            lat_all = dint("lat_all", [640, T], BF16)
            h_d = dint("h_d", [CW, T])
            halo = dint("halo_own", [CW, 256])
            halo_all = dint("halo_all", [2 * CW, 256])
            yc_d = dint("yc_d", [CW, T])
            mixT = dint("mixT_d", [D, T], BF16)
            y_d = dint("y_d", [D, T])
            outT = dout("outT", [D, T])
            sched = []
            for l in range(DEPTH):
                sched += wsched_a(w_in[l]) + wsched_b(w_out[l], w1[l], w2[l])
            k.ring_setup(sched)
            k.base = k.off
            for _ in range(3):
                k._wissue()
            rope_tables(k, pos)
            k.phase()
            X0, _ = k.alloc(16 * T, BF16)
            X0 = X0.rearrange("p (c t) -> p c t", c=16)
            scr = ln_scratch(k)
            xs = [[k.alloc(4 * 512, F32) for _ in range(4)] for _ in range(2)]

            def chunk0(c, t):
                a_, ak_ = xs[t][c // 4]
                return a_[:, (c % 4) * 512:(c % 4 + 1) * 512], ak_

            def loader0(c, t):
                if c % 4 == 0:
                    k.dma("sp", xs[t][c // 4][0].rearrange("p (c n) -> p c n", c=4),
                          xT[c * 128:(c + 4) * 128, t * 512:(t + 1) * 512].rearrange("(c p) n -> p c n", p=128),
                          [], [xs[t][c // 4][1]], f"lnx{t}_{c//4}")

            stats0 = ln_stats(k, chunk0, 16, LN_EPS, scr, loader=loader0)
            cnt = {"n": 0}

            def epi0(t, c0, grp):
                tsl = slice(t * 512, (t + 1) * 512)
                for i, (x, xk) in enumerate(grp):
                    c = c0 + i
                    k.act(X0[:, c, tsl], x, AF.Identity, [xk, k.vk], [f"X{c}"], scale=vcol(k, "ln_in_g", 0, c),
                          bias=vcol(k, "ln_in_b", 0, c))

                    if c % 2 == 1:
                        k.act(x, x, AF.Identity, [xk, k.vk], [xk], scale=vcol(k, "ln_in_g", 0, c), bias=vcol(k, "ln_in_b", 0, c))

                def post(t=t, c0=c0, grp=grp, tsl=tsl):
                    for i, (x, xk) in enumerate(grp):
                        c = c0 + i
                        if c % 2 == 0:
                            k.ts("dve", x, x, vcol(k, "ln_in_g", 0, c), vcol(k, "ln_in_b", 0, c), ALU.mult, ALU.add,
                                 [xk, k.vk], [xk])
                    k.dma("sp", XF[0][c0 * 128:(c0 + 4) * 128, tsl].rearrange("(c p) n -> p c n", p=128),
                          xs[t][c0 // 4][0].rearrange("p (c n) -> p c n", c=4), [xs[t][c0 // 4][1]], ["XF_a"],
                          f"lnst{cnt['n']%4}")
                    cnt["n"] += 1
                return post

            ln_norm(k, chunk0, stats0, 16, epi0)
            for l in range(DEPTH):
                phase_a(k, l, None, None, pos, w_in[l], cqn, lat, h_d, halo, x_in_sbuf=True)
                k.P.op("pool", lambda e: e.collective_compute("AllGather", ALU.bypass, replica_groups=PAIRS,
                                                              ins=[lat], outs=[lat_all]),
                       reads=["lat_own"], writes=["lat_all"], dma="cc_lat", inc=1)
                k.P.op("pool", lambda e: e.collective_compute("AllGather", ALU.bypass, replica_groups=PAIRS,
                                                              ins=[halo], outs=[halo_all]),
                       reads=["halo_own"], writes=["halo_all"], dma="cc_halo", inc=1)
                phase_conv(k, l, h_d, halo_all, yc_d, mixT)
                ATTN(k, l, cqn, lat_all, pos, maskf, w_uq[l], w_ukv[l], mixT)
                last = (l == DEPTH - 1)
                phase_out_ffn(k, l, mixT, XF[0], "XF_a", w_out[l], w1[l], w2[l],
                              outT if last else XF[0], "outT" if last else "XF_a")
            outs = ["outT"]
        k.P.barrier(final=True)
        k.P.emit()
    return nc


def _cols(v):
    v = np.asarray(v, np.float32).reshape(-1)
    return v.reshape(-1, 128).T


def pack_vecs(inp, rank, layers):
    cols = [_cols(inp["ln_in_g"]), _cols(inp["ln_in_b"])]
    invf = (10000.0 ** (-np.arange(0, ROPE, 2, dtype=np.float32) / ROPE)).astype(np.float32)
    c = np.zeros((128, 1), np.float32)
    c[0:32, 0] = invf
    c[32:64, 0] = invf
    cols.append(c)
    s = np.ones((128, 1), np.float32)
    s[0:32] = -1.0
    cols.append(s)
    cols.append(np.full((128, 1), 1.0 if rank == 0 else 0.0, np.float32))
    cols.append(np.full((128, 1), 1.0 if rank == 1 else 0.0, np.float32))
    for l in layers:
        cols.append(_cols(inp["g_q"][l]))
        cols.append(_cols(inp["g_kv"][l]))
        cols.append(_cols(inp["b_glu"][l]))
        wd = np.asarray(inp["w_dw"][l], np.float32)
        w = np.zeros((128, 8 * CK), np.float32)
        for cc in range(8):
            w[:, cc * CK:(cc + 1) * CK] = wd[:, cc * 128:(cc + 1) * 128].T
        cols.append(w)
        for nme in ["b_dw", "g_cln", "b_cln", "ln1_g", "ln1_b", "ln2_g", "ln2_b"]:
            cols.append(_cols(inp[nme][l]))
    return np.ascontiguousarray(np.concatenate(cols, axis=1))


def make_mask(rank):
    m = np.zeros((128, NBLK, 256), np.float32)
    diag = np.zeros((128, 128), np.float32)
    diag[0:64, 64:128] = -1e30
    own = slice(0, 128) if rank == 0 else slice(128, 256)
    oth = slice(128, 256) if rank == 0 else slice(0, 128)
    for j in range(NBLK):
        m[:, j, own] = diag
        m[:, j, oth] = -1e30 if rank == 0 else 0.0
    return np.ascontiguousarray(m.reshape(128, NBLK * 256))


def make_cst():
    c = np.zeros((128, 256), np.float32)
    c[:, 0:128] = np.eye(128, dtype=np.float32)
    c[:, 128:256] = 1.0
    return c


_PROGS = {}


def _prog(kind):
    if kind not in _PROGS:
        _PROGS[kind] = build(kind)
    return _PROGS[kind]


def _core_tokens(x, positions, c):
    b, r = c // 2, c % 2
    idx = np.concatenate([np.arange((2 * j + r) * 128, (2 * j + r + 1) * 128) for j in range(NBLK)])
    xT = np.ascontiguousarray(x[b][idx].T)
    pos = np.ascontiguousarray(np.broadcast_to(positions[b][idx].astype(np.int32)[None, :], (64, T)))
    return idx, xT, pos


FUSED = True


def kernel(**inp):
    inp = {k_: np.asarray(v) for k_, v in inp.items()}
    x = inp["x"].astype(np.float32, copy=False)
    positions = inp["positions"]
    cores = list(range(8))
    idxs, xTs, poss = zip(*[_core_tokens(x, positions, c) for c in cores])
    cst = make_cst()
    masks = [make_mask(c % 2) for c in cores]
    out = np.zeros((4, 2048, D), np.float32)
    if FUSED:
        ims = []
        for c in cores:
            ims.append({"vecs": pack_vecs(inp, c % 2, range(DEPTH)), "cst": cst, "xT": xTs[c], "pos": poss[c],
                        "maskf": masks[c], "w_in": inp["w_in"], "w_uq": inp["w_uq"], "w_ukv": inp["w_ukv"],
                        "w_out": inp["w_out"], "w1": inp["w1"], "w2": inp["w2"]})
        res = run_bass_kernel_spmd(_prog("FUSED"), ims, core_ids=cores).results
        for c in cores:
            out[c // 2][idxs[c]] = res[c]["outT"].T
        return out
    ims = [{"vecs": pack_vecs(inp, c % 2, [0]), "cst": cst, "xT": xTs[c]} for c in cores]
    res = run_bass_kernel_spmd(_prog("P0"), ims, core_ids=cores).results
    XF = [r["XF_o"] for r in res]
    XT_ = [r["XT_o"] for r in res]
    for l in range(DEPTH):
        vl = [pack_vecs(inp, c % 2, [l]) for c in cores]
        ims = [{"vecs": vl[c], "cst": cst, "XT_i": XT_[c], "pos": poss[c], "w_in": inp["w_in"][l]} for c in cores]
        ra = run_bass_kernel_spmd(_prog("A"), ims, core_ids=cores).results
        ims = []
        for c in cores:
            p0, p1 = (c // 2) * 2, (c // 2) * 2 + 1
            ims.append({"vecs": vl[c], "cst": cst, "XF_i": XF[c], "cqn_d": ra[c]["cqn_d"],
                        "lat_all": np.concatenate([ra[p0]["lat_own"], ra[p1]["lat_own"]], 0),
                        "h_d": ra[c]["h_d"],
                        "halo_all": np.concatenate([ra[p0]["halo_own"], ra[p1]["halo_own"]], 0),
                        "pos": poss[c], "maskf": masks[c], "w_uq": inp["w_uq"][l], "w_ukv": inp["w_ukv"][l],
                        "w_out": inp["w_out"][l], "w1": inp["w1"][l], "w2": inp["w2"][l]})
        rb = run_bass_kernel_spmd(_prog("B"), ims, core_ids=cores).results
        XF = [r["XF_o"] for r in rb]
        XT_ = [r["XT_o"] for r in rb]
    for c in cores:
        out[c // 2][idxs[c]] = XF[c].T
    return out
```

```python
import numpy as np
import ml_dtypes
import concourse.bass as bass
import concourse.mybir as mybir
from concourse.bass_utils import run_bass_kernel_spmd
from contextlib import ExitStack

F32 = mybir.dt.float32
BF16 = mybir.dt.bfloat16
I32 = mybir.dt.int32
AF = mybir.ActivationFunctionType
ALU = mybir.AluOpType
AX = mybir.AxisListType

D = 2048
T = 1024
NBLK = 8
DEPTH = 2
QL, KVL, ROPE = 512, 256, 64
NH = 8
CW = 1024
CK = 31
DFF = 8192
IN_COLS = QL + KVL + ROPE + 2 * CW
LN_EPS = 1e-5
RMS_EPS = 1e-6
ALPHA = (2.0 * DEPTH) ** 0.25
SCALE = (128 + 64) ** -0.5
TWO_PI = 2.0 * np.pi
BOUND_C = 16.0

ENGS = ["pe", "dve", "act", "pool", "sp"]
SAME_ENGINE_SYNC = {"dve", "act", "pool"}


class Prog:
    def __init__(self, nc, es):
        self.nc = nc
        self.es = es
        self.ops = {e: [] for e in ENGS}
        self.cnt = {e: 0 for e in ENGS}
        self.sems = {"E:" + e: es.enter_context(nc.semaphore("sem_" + e)) for e in ENGS}
        self.dcnt = {}
        self.last_w = {}
        self.readers = {}
        self.seen = {e: {} for e in ENGS}

    def _dsem(self, key):
        name = "D:" + key
        if name not in self.sems:
            self.sems[name] = self.es.enter_context(self.nc.semaphore("dsem_" + key))
            self.dcnt[name] = 0
        return name

    def op(self, eng, fns, reads=(), writes=(), dma=None, inc=16):
        if not isinstance(fns, (list, tuple)):
            fns = [fns]
        deps = set()
        for k in reads:
            if k in self.last_w:
                deps.add(self.last_w[k])
        for k in writes:
            if k in self.last_w:
                deps.add(self.last_w[k])
            for ev in self.readers.get(k, ()):
                deps.add(ev)
        need = {}
        for (s, v) in deps:
            if v > need.get(s, 0):
                need[s] = v
        waits = []
        for s, v in need.items():
            if s == "E:" + eng and dma is None and eng not in SAME_ENGINE_SYNC:
                continue
            if self.seen[eng].get(s, 0) >= v:
                continue
            self.seen[eng][s] = v
            waits.append((s, v))
        if dma is None:
            self.cnt[eng] += 1
            ev = ("E:" + eng, self.cnt[eng])
            inc = 1
        else:
            name = self._dsem(dma)
            self.dcnt[name] += inc * len(fns)
            ev = (name, self.dcnt[name])
        for k in writes:
            self.last_w[k] = ev
            self.readers[k] = []
        for k in reads:
            if k not in writes:
                self.readers.setdefault(k, []).append(ev)
        self.ops[eng].append((waits, fns, ev[0], inc, dma is not None))
        return ev

    def barrier(self, final=False):
        evs = [("E:" + e, self.cnt[e]) for e in ENGS if self.cnt[e] > 0]
        evs += [(n, v) for n, v in self.dcnt.items() if v > 0 and (final or not n.startswith("D:wslot"))]
        for e in ENGS:
            waits = []
            for (s, v) in evs:
                if s == "E:" + e:
                    continue
                if self.seen[e].get(s, 0) >= v:
                    continue
                self.seen[e][s] = v
                waits.append((s, v))
            if waits:
                self.ops[e].append((waits, [], None, 0, False))
        self.last_w = {k_: v for k_, v in self.last_w.items() if k_.startswith("wslot")}
        self.readers = {k_: v for k_, v in self.readers.items() if k_.startswith("wslot")}

    def emit(self):
        nc = self.nc
        with nc.Block() as block:
            def run(e, eng):
                for (waits, fns, sname, inc, is_dma) in self.ops[e]:
                    for (s, v) in waits:
                        eng.wait_ge(self.sems[s], v)
                    for i, f in enumerate(fns):
                        ins = f(eng)
                        if is_dma or i == len(fns) - 1:
                            ins.then_inc(self.sems[sname], inc)

            @block.tensor
            def _(eng):
                run("pe", eng)

            @block.vector
            def _(eng):
                run("dve", eng)

            @block.scalar
            def _(eng):
                run("act", eng)

            @block.gpsimd
            def _(eng):
                run("pool", eng)

            @block.sync
            def _(eng):
                run("sp", eng)


def _vec_layout():
    g = {}
    off = 0
    for name, n in [("ln_in_g", 16), ("ln_in_b", 16), ("inv_freq", 1), ("sgn", 1), ("s_a", 1), ("s_b", 1)]:
        g[name] = off
        off += n
    gl = off
    l = {}
    off = 0
    for name, n in [("g_q", 4), ("g_kv", 2), ("b_glu", 16), ("w_dw", 8 * CK), ("b_dw", 8), ("g_cln", 8),
                    ("b_cln", 8), ("ln1_g", 16), ("ln1_b", 16), ("ln2_g", 16), ("ln2_b", 16)]:
        l[name] = off
        off += n
    return g, gl, l, off


VG, NVG, VL, NVL = _vec_layout()
NV = NVG + DEPTH * NVL


class K:
    def __init__(self, nc, es, mode):
        self.nc = nc
        self.es = es
        self.P = Prog(nc, es)
        self.mode = mode
        self.POOLW = 53000
        self.pool = es.enter_context(nc.sbuf_tensor("pool", [128, self.POOLW], F32))
        self.ps = es.enter_context(nc.psum_tensor("ps", [128, 4096], F32))
        self.off = 0
        self.base = 0
        self.uid = 0
        self.rr = 0
        self.ring = None

    def alloc(self, nelem, dtype, parts=128):
        nbytes = nelem * (2 if dtype == BF16 else 4)
        words = (nbytes + 31) // 32 * 8
        assert self.off + words <= self.POOLW, ("SBUF pool overflow", self.off, words)
        a = self.pool[0:parts, self.off:self.off + words]
        self.off += words
        if dtype != F32:
            a = a.bitcast(dtype)
        a = a[:, 0:nelem]
        self.uid += 1
        return a, f"b{self.uid}"

    def phase(self):
        self.P.barrier()
        self.off = self.base

    def bank(self, b=None):
        if b is None:
            b = self.rr
            self.rr = (self.rr + 1) % 8
        return self.ps[:, b * 512:(b + 1) * 512], f"ps{b}"

    def dma(self, eng, out, in_, reads, writes, key):
        return self.P.op(eng, lambda e: e.dma_start(out=out, in_=in_), reads=reads, writes=writes, dma=key)

    def mm(self, out, pairs, reads, writes, first=True, last=True):
        n = len(pairs)
        fns = []
        for i, (l, r) in enumerate(pairs):
            fns.append(lambda e, l=l, r=r, i=i: e.matmul(out, lhsT=l, rhs=r, start=(first and i == 0),
                                                         stop=(last and i == n - 1)))
        return self.P.op("pe", fns, reads=reads, writes=writes)

    def act(self, out, in_, func, reads, writes, scale=1.0, bias=0.0):
        return self.P.op("act", lambda e: e.activation(out=out, in_=in_, func=func, bias=bias, scale=scale),
                         reads=reads, writes=writes)

    def tt(self, eng, out, in0, in1, op, reads, writes):
        return self.P.op(eng, lambda e: e.tensor_tensor(out=out, in0=in0, in1=in1, op=op), reads=reads, writes=writes)

    def ts(self, eng, out, in0, s1, s2, op0, op1, reads, writes):
        if op1 is None:
            return self.P.op(eng, lambda e: e.tensor_scalar(out=out, in0=in0, scalar1=s1, scalar2=None, op0=op0),
                             reads=reads, writes=writes)
        return self.P.op(eng, lambda e: e.tensor_scalar(out=out, in0=in0, scalar1=s1, scalar2=s2, op0=op0, op1=op1),
                         reads=reads, writes=writes)

    def stt(self, out, in0, scalar, in1, op0, op1, reads, writes):
        return self.P.op("dve", lambda e: e.scalar_tensor_tensor(out=out, in0=in0, scalar=scalar, in1=in1,
                                                                 op0=op0, op1=op1), reads=reads, writes=writes)

    def ring_setup(self, sched, nslots=4, slot_elems=8192):
        self.ring = [self.alloc(slot_elems, BF16) for _ in range(nslots)]
        self.wsched = sched
        self.w_issued = 0
        self.w_taken = 0

    def _wissue(self):
        i = self.w_issued
        src2d, nkc, ncols = self.wsched[i]
        ap, _ = self.ring[i % len(self.ring)]
        key = f"wslot{i % len(self.ring)}"
        view = ap[:, 0:nkc * ncols].rearrange("p (c n) -> p c n", c=nkc)
        src = src2d.rearrange("(c p) n -> p c n", p=128)
        self.dma("pool", view, src, [], [key], key)
        self.w_issued += 1

    def ring_init(self, *a, **kw):
        pass

    def wload(self, src2d, nkc, ncols):
        c = self.w_taken
        s2, n2, c2 = self.wsched[c]
        assert (n2, c2) == (nkc, ncols), ("weight schedule mismatch", c)
        while self.w_issued < min(len(self.wsched), c + len(self.ring) - 1):
            self._wissue()
        ap, _ = self.ring[c % len(self.ring)]
        key = f"wslot{c % len(self.ring)}"
        self.w_taken += 1
        return ap[:, 0:nkc * ncols].rearrange("p (c n) -> p c n", c=nkc), key


def load_consts(k, vecs_d, cst_d, nv):
    k.vecs, k.vk = k.alloc(nv, F32)
    k.dma("sp", k.vecs, vecs_d, [], [k.vk], "vecs")
    cst, ck = k.alloc(256, F32)
    k.dma("sp", cst, cst_d, [], [ck], "cst")
    k.ident_f, k.ones_f, k.ck = cst[:, 0:128], cst[:, 128:256], ck
    k.ident_b, k.ibk = k.alloc(128, BF16)
    k.P.op("dve", lambda e: e.tensor_copy(out=k.ident_b, in_=k.ident_f), reads=[ck], writes=[k.ibk])
    k.base = k.off


def vcol_idx(k, name, layer):
    if name in VG:
        return VG[name]
    return NVG + (layer if k.mode == "FUSED" else 0) * NVL + VL[name]


def vcol(k, name, layer, i=0):
    if name in VG:
        c = VG[name] + i
    else:
        c = NVG + (layer if k.mode == "FUSED" else 0) * NVL + VL[name] + i
    return k.vecs[:, c:c + 1]


def rstd_from_sums(k, psA, ka, psB, kb, n, eps, tag):
    rstd, rk = k.alloc(512, F32)
    mean, mk = (None, None)
    if psA is not None:
        mean, mk = k.alloc(512, F32)
        msq, qk = k.alloc(512, F32)
        k.act(mean, psA, AF.Copy, [ka], [mk], scale=1.0 / n)
        k.act(msq, mean, AF.Square, [mk], [qk])
        k.stt(rstd, psB, 1.0 / n, msq, ALU.mult, ALU.subtract, [kb, qk], [rk])
        k.act(rstd, rstd, AF.Sqrt, [rk], [rk], bias=eps)
    else:
        k.act(rstd, psB, AF.Sqrt, [kb], [rk], scale=1.0 / n, bias=eps)
    k.P.op("dve", lambda e: e.reciprocal(out=rstd, in_=rstd), reads=[rk], writes=[rk])
    return mean, mk, rstd, rk


def lnfm(k, src_d, sk, C, gname, bname, layer, eps, func, dst_f=None, dfk=None, dst_b=None, dbk=None, res=None):
    P = k.P
    N = C * 128
    NT = T // 512
    if res is None:
        xs = [[k.alloc(4 * 512, F32) for _ in range(C // 4)] for _ in range(NT)]
    sq = [k.alloc(512, F32) for _ in range(2)]

    def chunk(c, t):
        if res is not None:
            return res[0][:, c, t * 512:(t + 1) * 512], res[1][c]
        a_, ak_ = xs[t][c // 4]
        return a_[:, (c % 4) * 512:(c % 4 + 1) * 512], ak_

    stats = []
    n = 0
    for t in range(NT):
        tsl = slice(t * 512, (t + 1) * 512)
        psA, ka = k.bank()
        psB, kb = k.bank()
        for c in range(C):
            x, xk = chunk(c, t)
            if res is None and c % 4 == 0:
                k.dma("sp", xs[t][c // 4][0].rearrange("p (c n) -> p c n", c=4),
                      src_d[c * 128:(c + 4) * 128, tsl].rearrange("(c p) n -> p c n", p=128), [sk], [xk], f"lnx{t}_{c//4}")
            s, sqk = sq[n % 2]
            n += 1
            k.act(s, x, AF.Square, [xk], [sqk])
            k.mm(psA, [(k.ones_f, x)], [k.ck, xk], [ka], first=(c == 0), last=(c == C - 1))
            k.mm(psB, [(k.ones_f, s)], [k.ck, sqk], [kb], first=(c == 0), last=(c == C - 1))
        stats.append(rstd_from_sums(k, psA, ka, psB, kb, N, eps, "ln"))
    n = 0
    o16g = [k.alloc(4 * 512, BF16) for _ in range(3)] if dst_b is not None else None
    for t in range(NT):
        tsl = slice(t * 512, (t + 1) * 512)
        mean, mk, rstd, rk = stats[t]
        for c0 in range(0, C, 4):
            grp = [chunk(c, t) for c in range(c0, c0 + 4)]
            for (x, xk) in grp:
                k.tt("dve", x, x, mean, ALU.subtract, [xk, mk], [xk])
            for (x, xk) in grp:
                k.tt("dve", x, x, rstd, ALU.mult, [xk, rk], [xk])
            if dst_f is not None:
                if dst_b is not None:
                    ob, obk = o16g[n % 3]
                    for i, (x, xk) in enumerate(grp):
                        k.act(ob[:, i * 512:(i + 1) * 512], x, func, [xk, k.vk], [obk],
                              scale=vcol(k, gname, layer, c0 + i), bias=vcol(k, bname, layer, c0 + i))
                    k.dma("sp", dst_b[c0 * 128:(c0 + 4) * 128, tsl].rearrange("(c p) n -> p c n", p=128),
                          ob.rearrange("p (c n) -> p c n", c=4), [obk], [dbk], f"lnsb{n%4}")
                for i, (x, xk) in enumerate(grp):
                    k.act(x, x, func, [xk, k.vk], [xk], scale=vcol(k, gname, layer, c0 + i), bias=vcol(k, bname, layer, c0 + i))
                if res is not None:
                    src = res[0][:, c0:c0 + 4, tsl]
                else:
                    src = xs[t][c0 // 4][0].rearrange("p (c n) -> p c n", c=4)
                k.dma("sp", dst_f[c0 * 128:(c0 + 4) * 128, tsl].rearrange("(c p) n -> p c n", p=128), src,
                      list({xk for (_, xk) in grp}), [dfk], f"lnst{n%4}")
            else:
                ob, obk = o16g[n % 3]
                for i, (x, xk) in enumerate(grp):
                    k.act(ob[:, i * 512:(i + 1) * 512], x, func, [xk, k.vk], [obk], scale=vcol(k, gname, layer, c0 + i),
                          bias=vcol(k, bname, layer, c0 + i))
                k.dma("sp", dst_b[c0 * 128:(c0 + 4) * 128, tsl].rearrange("(c p) n -> p c n", p=128),
                      ob.rearrange("p (c n) -> p c n", c=4), [obk], [dbk], f"lnst{n%4}")
            n += 1


def ln_scratch(k):
    NT = T // 512
    return {"sq": [k.alloc(512, F32) for _ in range(2)],
            "mean": [k.alloc(512, F32) for _ in range(NT)],
            "rstd": [k.alloc(512, F32) for _ in range(NT)]}


def ln_stats(k, chunk, C, eps, scr, loader=None):
    N = C * 128
    stats = []
    n = 0
    for t in range(T // 512):
        psA, ka = k.bank()
        psB, kb = k.bank()
        for c in range(C):
            if loader is not None:
                loader(c, t)
            x, xk = chunk(c, t)
            s, sqk = scr["sq"][n % 2]
            n += 1
            k.act(s, x, AF.Square, [xk], [sqk])
            k.mm(psA, [(k.ones_f, x)], [k.ck, xk], [ka], first=(c == 0), last=(c == C - 1))
            k.mm(psB, [(k.ones_f, s)], [k.ck, sqk], [kb], first=(c == 0), last=(c == C - 1))
        mean, mk = scr["mean"][t]
        rstd, rk = scr["rstd"][t]

        def fin(psA=psA, ka=ka, psB=psB, kb=kb, mean=mean, mk=mk, rstd=rstd, rk=rk):
            k.act(mean, psA, AF.Copy, [ka], [mk], scale=1.0 / N)
            k.act(rstd, mean, AF.Square, [mk], [rk])
            k.stt(rstd, psB, 1.0 / N, rstd, ALU.mult, ALU.subtract, [kb, rk], [rk])
            k.act(rstd, rstd, AF.Sqrt, [rk], [rk], bias=eps)
            k.P.op("dve", lambda e, rstd=rstd: e.reciprocal(out=rstd, in_=rstd), reads=[rk], writes=[rk])

        if t == T // 512 - 1:
            k.ln_fin = fin
        else:
            fin()
        stats.append((mean, mk, rstd, rk))
    return stats


def ln_norm(k, chunk, stats, C, epilogue):
    pending = [None]
    for t in range(T // 512):
        if t == T // 512 - 1 and getattr(k, "ln_fin", None) is not None:
            k.ln_fin()
            k.ln_fin = None
        mean, mk, rstd, rk = stats[t]
        for c0 in range(0, C, 4):
            grp = [chunk(c, t) for c in range(c0, c0 + 4)]
            for (x, xk) in grp:
                k.tt("dve", x, x, mean, ALU.subtract, [xk, mk], [xk])
            for (x, xk) in grp:
                k.tt("dve", x, x, rstd, ALU.mult, [xk, rk], [xk])
            if pending[0] is not None:
                pending[0]()
            pending[0] = epilogue(t, c0, grp)
    if pending[0] is not None:
        pending[0]()


def rope_tables(k, pos_d):
    if getattr(k, "Ct", None) is not None:
        return
    P = k.P
    Ct, Ck = k.alloc(T, F32, parts=64)
    St, Sk = k.alloc(T, F32, parts=64)
    k.base = k.off
    posi, pk = k.alloc(T, I32, parts=64)
    k.dma("sp", posi, pos_d, [], [pk], "posi")
    ang, ak = k.alloc(T, F32, parts=64)
    tmp, tk = k.alloc(T, F32, parts=64)
    ki, kik = k.alloc(T, I32, parts=64)
    invf = k.vecs[0:64, VG["inv_freq"]:VG["inv_freq"] + 1]
    sgn = k.vecs[0:64, VG["sgn"]:VG["sgn"] + 1]
    P.op("dve", lambda e: e.tensor_copy(out=ang, in_=posi), reads=[pk], writes=[ak])
    k.ts("dve", ang, ang, invf, None, ALU.mult, None, [ak, k.vk], [ak])

    def reduce_sin(dst, dk, shift, scale_ap):
        k.ts("dve", tmp, ang, shift, 1.0 / TWO_PI, ALU.add, ALU.mult, [ak], [tk])
        P.op("dve", lambda e: e.tensor_copy(out=ki, in_=tmp), reads=[tk], writes=[kik])
        P.op("dve", lambda e: e.tensor_copy(out=tmp, in_=ki), reads=[kik], writes=[tk])
        k.ts("dve", tmp, tmp, -TWO_PI, shift, ALU.mult, ALU.add, [tk], [tk])
        k.tt("dve", tmp, tmp, ang, ALU.add, [tk, ak], [tk])
        k.ts("dve", dst, tmp, float(np.pi), -TWO_PI, ALU.is_gt, ALU.mult, [tk], [dk])
        k.tt("dve", tmp, tmp, dst, ALU.add, [tk, dk], [tk])
        k.ts("dve", dst, tmp, float(-np.pi), TWO_PI, ALU.is_lt, ALU.mult, [tk], [dk])
        k.tt("dve", tmp, tmp, dst, ALU.add, [tk, dk], [tk])
        k.ts("dve", tmp, tmp, 3.14159, -3.14159, ALU.min, ALU.max, [tk], [tk])
        k.act(dst, tmp, AF.Sin, [tk, k.vk], [dk], scale=scale_ap)

    reduce_sin(St, Sk, 0.0, sgn)
    reduce_sin(Ct, Ck, float(np.pi / 2), 1.0)
    k.Ct, k.Ck, k.St, k.Sk = Ct, Ck, St, Sk


def rope_scratch(k, n):
    return k.alloc(n, F32, parts=64), k.alloc(n, F32, parts=64)


def rope_apply(k, xf, xfk, dst, dk, sl, prescale, tag, scr):
    (sw, swk), (t1, t1k) = scr
    k.dma("sp", sw[0:32, :], xf[32:64, :], [xfk], [swk], f"sw{tag}a")
    k.dma("sp", sw[32:64, :], xf[0:32, :], [xfk], [swk + "b"], f"sw{tag}b")
    k.stt(t1, xf, prescale, k.Ct[:, sl], ALU.mult, ALU.mult, [xfk, k.Ck], [t1k])
    k.stt(sw, sw, prescale, k.St[:, sl], ALU.mult, ALU.mult, [swk, swk + "b", k.Sk], [swk, swk + "b"])
    k.tt("dve", dst, t1, sw, ALU.add, [t1k, swk, swk + "b"], [dk])


def phase_a(k, layer, XT_d, xtk, pos_d, w_in, cqn_d, lat_own, h_d, halo_own, x_in_sbuf=False):
    P = k.P
    k.phase()
    rope_tables(k, pos_d)
    xt, _ = k.alloc(16 * T, BF16)
    xt = xt.rearrange("p (c t) -> p c t", c=16)
    xks = []
    for q in range(0 if x_in_sbuf else 4):
        kk = f"xtq{q}"
        k.dma("sp", xt[:, q * 4:(q + 1) * 4, :], XT_d[q * 512:(q + 1) * 512, :].rearrange("(c p) t -> p c t", p=128),
              [xtk], [kk], kk)
        xks.append(kk)
    k.ring_init(4, 8192)

    cf = [k.alloc(512, F32) for _ in range(4)]
    sq = [k.alloc(512, F32) for _ in range(2)]
    ob = [k.alloc(512, BF16) for _ in range(2)]

    def norm_proj(panel, pkey, col0, noc, gname, nfeat, dst_d, dkey, row0):
        for t in range(2):
            tsl = slice(t * 512, (t + 1) * 512)
            psB, kb = k.bank()
            for oc in range(noc):
                ps, pk = k.bank()
                k.mm(ps, [(panel[:, kc, col0 + oc * 128:col0 + (oc + 1) * 128], xt[:, kc, tsl]) for kc in range(16)],
                     [pkey] + xks, [pk])
                c, ckk = cf[oc]
                k.act(c, ps, AF.Copy, [pk], [ckk])
                s, sk = sq[oc % 2]
                k.act(s, ps, AF.Square, [pk], [sk])
                k.mm(psB, [(k.ones_f, s)], [k.ck, sk], [kb], first=(oc == 0), last=(oc == noc - 1))
            _, _, rstd, rk = rstd_from_sums(k, None, None, psB, kb, nfeat, RMS_EPS, "rms")
            for oc in range(noc):
                c, ckk = cf[oc]
                o, ok = ob[oc % 2]
                k.stt(o, c, vcol(k, gname, layer, oc), rstd, ALU.mult, ALU.mult, [ckk, rk, k.vk], [ok])
                k.dma("sp", dst_d[row0 + oc * 128:row0 + (oc + 1) * 128, tsl], o, [ok], [dkey], f"npo{oc%2}")

    pan, pkey = k.wload(w_in[:, 0:512], 16, 512)
    norm_proj(pan, pkey, 0, 4, "g_q", QL, cqn_d, "cqn_d", 0)
    pan, pkey = k.wload(w_in[:, 512:832], 16, 320)
    norm_proj(pan, pkey, 0, 2, "g_kv", KVL, lat_own, "lat_own", 0)
    kscr = rope_scratch(k, 512)
    krf, kfk = k.alloc(512, F32, parts=64)
    krb, kbk = k.alloc(512, BF16, parts=64)
    for t in range(2):
        tsl = slice(t * 512, (t + 1) * 512)
        ps, pk = k.bank()
        k.mm(ps[0:64, :], [(pan[:, kc, 256:320], xt[:, kc, tsl]) for kc in range(16)], [pkey] + xks, [pk])
        k.act(krf, ps[0:64, :], AF.Copy, [pk], [kfk])
        rope_apply(k, krf, kfk, krb, kbk, tsl, 1.0, "k", kscr)
        k.dma("sp", lat_own[256:320, tsl], krb, [kbk], ["lat_own"], "krb")
    hs = [k.alloc(512, F32) for _ in range(2)]
    sg = [k.alloc(512, F32) for _ in range(2)]
    n = 0
    for hp in range(2):
        pa, pak = k.wload(w_in[:, 832 + hp * 512:832 + (hp + 1) * 512], 16, 512)
        pg, pgk = k.wload(w_in[:, 832 + 1024 + hp * 512:832 + 1024 + (hp + 1) * 512], 16, 512)
        for c4 in range(4):
            cc = hp * 4 + c4
            for t in range(2):
                tsl = slice(t * 512, (t + 1) * 512)
                psa, ka = k.bank()
                psg, kg = k.bank()
                k.mm(psa, [(pa[:, kc, c4 * 128:(c4 + 1) * 128], xt[:, kc, tsl]) for kc in range(16)], [pak] + xks, [ka])
                k.mm(psg, [(pg[:, kc, c4 * 128:(c4 + 1) * 128], xt[:, kc, tsl]) for kc in range(16)], [pgk] + xks, [kg])
                s, sk = sg[n % 2]
                h, hk = hs[n % 2]
                k.act(s, psg, AF.Sigmoid, [kg, k.vk], [sk], bias=vcol(k, "b_glu", layer, 8 + cc))
                k.stt(h, psa, vcol(k, "b_glu", layer, cc), s, ALU.add, ALU.mult, [ka, sk, k.vk], [hk])
                k.dma("sp", h_d[cc * 128:(cc + 1) * 128, tsl], h, [hk], ["h_d"], f"hst{n%2}")
                k.dma("sp", halo_own[cc * 128:(cc + 1) * 128, t * 128:(t + 1) * 128].rearrange("p (b f) -> p b f", b=4),
                      h.rearrange("p (b f) -> p b f", b=4)[:, :, 96:128], [hk], ["halo_own"], f"hhalo{n%2}")
                n += 1


def phase_conv(k, layer, h_d, halo_all, yc_d, mixT_d):
    P = k.P
    k.phase()
    hb = [k.alloc(NBLK * 160, BF16) for _ in range(3)]
    g0 = [k.alloc(256, F32) for _ in range(3)]
    g1 = [k.alloc(256, F32) for _ in range(3)]
    dg = [k.alloc(CK * 128, BF16) for _ in range(3)]
    ycres, _ = k.alloc(8 * T, F32)
    ycres = ycres.rearrange("p (c t) -> p c t", c=8)
    yks = [f"ycres{c}" for c in range(8)]
    scr = ln_scratch(k)
    o16g = [k.alloc(4 * 512, BF16) for _ in range(3)]
    s_a = k.vecs[:, VG["s_a"]:VG["s_a"] + 1]
    s_b = k.vecs[:, VG["s_b"]:VG["s_b"] + 1]
    def prep(cc):
        h, hk = hb[cc % 3]
        h3 = h.rearrange("p (b f) -> p b f", b=NBLK)
        a0, a0k = g0[cc % 3]
        a1, a1k = g1[cc % 3]
        d, dk = dg[cc % 3]
        rows = slice(cc * 128, (cc + 1) * 128)
        k.dma("pool", h3[:, :, 32:160], h_d[rows, :].rearrange("p (b f) -> p b f", b=NBLK), ["h_d"], [hk + "m"], f"cvh{cc%3}")
        k.dma("sp", a0, halo_all[rows, :], ["halo_all"], [a0k], f"cvg0{cc%3}")
        k.dma("sp", a1, halo_all[1024 + cc * 128:1024 + (cc + 1) * 128, :], ["halo_all"], [a1k], f"cvg1{cc%3}")
        P.op("dve", lambda e, h3=h3: e.memset(h3[:, 0:1, 0:32], 0.0), writes=[hk + "h"])
        k.ts("dve", h3[:, 1:8, 0:32], a1.rearrange("p (b f) -> p b f", b=NBLK)[:, 0:7, :], s_a, None, ALU.mult, None,
             [a1k, k.vk], [hk + "h"])
        k.stt(h3[:, :, 0:32], a0.rearrange("p (b f) -> p b f", b=NBLK), s_b, h3[:, :, 0:32], ALU.mult, ALU.add,
              [a0k, k.vk, hk + "h"], [hk + "h"])
        for tap in range(CK):
            w = vcol(k, "w_dw", layer, cc * CK + tap)
            dst = d[:, tap * 128:(tap + 1) * 128]
            if tap % 2 == 0:
                k.ts("dve", dst, k.ident_f, w, None, ALU.mult, None, [k.ck, k.vk], [dk + f"_{tap}"])
            else:
                k.act(dst, k.ident_f, AF.Copy, [k.ck, k.vk], [dk + f"_{tap}"], scale=w)

    n = 0
    prep(0)
    for cc in range(8):
        if cc + 1 < 8:
            prep(cc + 1)
        h, hk = hb[cc % 3]
        h3 = h.rearrange("p (b f) -> p b f", b=NBLK)
        d, dk = dg[cc % 3]
        rows = slice(cc * 128, (cc + 1) * 128)
        dkeys = [dk + f"_{tap}" for tap in range(CK)]
        for half in range(2):
            ps, pk = k.bank()
            ps3 = ps.rearrange("p (b f) -> p b f", b=4)
            fns = [lambda e, tap=tap, ps3=ps3, h3=h3, d=d, half=half: e.matmul(
                ps3, lhsT=d[:, tap * 128:(tap + 1) * 128], rhs=h3[:, half * 4:(half + 1) * 4, 2 + tap:130 + tap],
                start=(tap == 0), stop=(tap == CK - 1)) for tap in range(CK)]
            P.op("pe", fns, reads=dkeys + [hk + "m", hk + "h"], writes=[pk])
            k.act(ycres[:, cc, half * 512:(half + 1) * 512], ps, AF.Identity, [pk, k.vk], [yks[cc]],
                  bias=vcol(k, "b_dw", layer, cc))
            n += 1
    def chunk(c, t):
        return ycres[:, c, t * 512:(t + 1) * 512], yks[c]

    stats = ln_stats(k, chunk, 8, LN_EPS, scr)
    cnt = {"n": 0}

    def epi(t, c0, grp):
        tsl = slice(t * 512, (t + 1) * 512)
        ob, obk = o16g[cnt["n"] % 3]
        for i, (x, xk) in enumerate(grp):
            k.act(ob[:, i * 512:(i + 1) * 512], x, AF.Silu, [xk, k.vk], [obk], scale=vcol(k, "g_cln", layer, c0 + i),
                  bias=vcol(k, "b_cln", layer, c0 + i))
        k.dma("sp", mixT_d[1024 + c0 * 128:1024 + (c0 + 4) * 128, tsl].rearrange("(c p) n -> p c n", p=128),
              ob.rearrange("p (c n) -> p c n", c=4), [obk], ["mixT_d"], f"lnst{cnt['n']%4}")
        cnt["n"] += 1

    ln_norm(k, chunk, stats, 8, epi)


def phase_attn(k, layer, cqn_d, lat_all, pos_d, mask_d, w_uq, w_ukv, mixT_d):
    P = k.P
    k.phase()
    rope_tables(k, pos_d)
    mf, mfk = k.alloc(NBLK * 256, F32)
    k.dma("sp", mf, mask_d, [], [mfk], "maskf")
    mb, mbk = k.alloc(NBLK * 256, BF16)
    P.op("dve", lambda e: e.tensor_copy(out=mb, in_=mf), reads=[mfk], writes=[mbk])
    mb3 = mb.rearrange("p (j c) -> p j c", j=NBLK)
    cqn, cqk = k.alloc(4 * T, BF16)
    cqn = cqn.rearrange("p (c t) -> p c t", c=4)
    k.dma("sp", cqn, cqn_d.rearrange("(c p) t -> p c t", p=128), ["cqn_d"], [cqk], "cqn")
    lat, _ = k.alloc(2 * 2048, BF16)
    lat = lat.rearrange("p (c t) -> p c t", c=2)
    kra, _ = k.alloc(2048, BF16, parts=64)
    lks = []
    for r in range(2):
        for kc in range(2):
            kk = f"lat{r}{kc}"
            k.dma("sp", lat[:, kc, r * 1024:(r + 1) * 1024], lat_all[r * 320 + kc * 128:r * 320 + (kc + 1) * 128, :],
                  ["lat_all"], [kk], kk)
            lks.append(kk)
        kk = f"kra{r}"
        k.dma("sp", kra[:, r * 1024:(r + 1) * 1024], lat_all[r * 320 + 256:r * 320 + 320, :], ["lat_all"], [kk], kk)
        lks.append(kk)
    wq = [k.alloc(4 * 192, BF16) for _ in range(2)]
    wkv = [k.alloc(2 * 256, BF16) for _ in range(2)]
    qn = [k.alloc(T, BF16) for _ in range(2)]
    qr = [k.alloc(T, BF16, parts=64) for _ in range(2)]
    qrf = [k.alloc(T, F32, parts=64) for _ in range(2)]
    kT = [k.alloc(2048, BF16) for _ in range(2)]
    Vh = [k.alloc(16 * 129, BF16) for _ in range(2)]
    mixh = [k.alloc(T, BF16) for _ in range(2)]
    Pb = [k.alloc(2048, BF16) for _ in range(2)]
    PT = [k.alloc(2048, BF16) for _ in range(2)]
    att = [k.alloc(128, BF16) for _ in range(2)]
    mx = [k.alloc(1, F32) for _ in range(2)]
    rs = [k.alloc(1, F32) for _ in range(2)]
    for i in range(2):
        v3 = Vh[i][0].rearrange("p (b f) -> p b f", b=16)
        P.op("pool", lambda e, v3=v3: e.memset(v3[:, :, 128:129], 1.0), writes=[Vh[i][1] + "o"])
    qscr = rope_scratch(k, T)
    psT = [(k.ps[:, 4 * 512:5 * 512].bitcast(BF16), "ps4"), (k.ps[:, 5 * 512:6 * 512].bitcast(BF16), "ps5")]
    psO, kO = k.ps[:, 6 * 512:7 * 512], "ps6"
    psM, kM = k.ps[:, 7 * 512:8 * 512], "ps7"
    psMb = psM.bitcast(BF16)
    state = {"pt": 0}

    def head_setup(h):
        b = h % 2
        wqa, wqk = wq[b]
        wq3 = wqa.rearrange("p (c n) -> p c n", c=4)
        k.dma("pool", wq3, w_uq[:, h * 192:(h + 1) * 192].rearrange("(c p) n -> p c n", p=128), [], [wqk], f"wq{b}")
        wka, wkk = wkv[b]
        wk3 = wka.rearrange("p (c n) -> p c n", c=2)
        k.dma("pool", wk3, w_ukv[:, h * 256:(h + 1) * 256].rearrange("(c p) n -> p c n", p=128), [], [wkk], f"wkv{b}")
        qna, qnk = qn[b]
        qfa, qfk = qrf[b]
        for t in range(2):
            tsl = slice(t * 512, (t + 1) * 512)
            k.mm(psM, [(wq3[:, kc, 0:128], cqn[:, kc, tsl]) for kc in range(4)], [wqk, cqk], [kM])
            k.act(qna[:, tsl], psM, AF.Copy, [kM], [qnk], scale=SCALE)
            k.mm(psM[0:64, :], [(wq3[:, kc, 128:192], cqn[:, kc, tsl]) for kc in range(4)], [wqk, cqk], [kM])
            k.act(qfa[:, tsl], psM[0:64, :], AF.Copy, [kM], [qfk])
        rope_apply(k, qfa, qfk, qr[b][0], qr[b][1], slice(0, T), SCALE, "q", qscr)
        kTa, kTk = kT[b]
        for t in range(4):
            tsl = slice(t * 512, (t + 1) * 512)
            k.mm(psM, [(wk3[:, kc, 0:128], lat[:, kc, tsl]) for kc in range(2)], [wkk] + lks, [kM])
            k.act(kTa[:, tsl], psM, AF.Copy, [kM], [kTk])
        va, vk = Vh[b]
        v3 = va.rearrange("p (b f) -> p b f", b=16)
        for g in range(4):
            fns = []
            for i in range(4):
                kb = g * 4 + i
                for kc in range(2):
                    fns.append(lambda e, i=i, kb=kb, kc=kc: e.matmul(psM[:, i * 128:(i + 1) * 128],
                                                                     lhsT=lat[:, kc, kb * 128:(kb + 1) * 128],
                                                                     rhs=wk3[:, kc, 128:256], start=(kc == 0), stop=(kc == 1)))
            P.op("pe", fns, reads=[wkk] + lks, writes=[kM])
            P.op("dve", lambda e, g=g, v3=v3: e.tensor_copy(out=v3[:, g * 4:(g + 1) * 4, 0:128],
                                                           in_=psM.rearrange("p (b f) -> p b f", b=4)),
                 reads=[kM], writes=[vk])

    def score_banks(j):
        ncol = 2 * (j + 1) * 128
        if ncol <= 1024:
            return (j % 2) * 2
        return 0

    def qk(h, j):
        b = h % 2
        nA = (j + 1) * 128
        ncol = 2 * nA
        b0 = score_banks(j)
        fns = []
        wkeys = []
        nseg = (ncol + 511) // 512
        for s in range(nseg):
            lo, hi = s * 512, min((s + 1) * 512, ncol)
            bankoff = (b0 + s) * 512
            wkeys.append(f"ps{b0 + s}")
            for (plo, phi, kbase) in ((0, nA, 0), (nA, ncol, 1024 - nA)):
                a, z = max(lo, plo), min(hi, phi)
                if a >= z:
                    continue
                out = k.ps[:, bankoff + (a - lo):bankoff + (z - lo)]
                kcols = slice(kbase + a, kbase + z)
                fns.append(lambda e, out=out, kcols=kcols: e.matmul(out, lhsT=qn[b][0][:, j * 128:(j + 1) * 128],
                                                                    rhs=kT[b][0][:, kcols], start=True, stop=False))
                mlo = phi - 128
                has_mask = (a <= mlo and z >= phi)
                fns.append(lambda e, out=out, kcols=kcols, hm=has_mask: e.matmul(
                    out, lhsT=qr[b][0][:, j * 128:(j + 1) * 128], rhs=kra[:, kcols], start=False, stop=(not hm)))
                if has_mask:
                    mout = k.ps[:, bankoff + (mlo - lo):bankoff + (phi - lo)]
                    mcol = 0 if plo == 0 else 128
                    fns.append(lambda e, mout=mout, mcol=mcol: e.matmul(mout, lhsT=k.ident_b,
                                                                        rhs=mb3[:, j, mcol:mcol + 128],
                                                                        start=False, stop=True))
        P.op("pe", fns, reads=[qn[b][1], qr[b][1], kT[b][1], mbk, k.ibk] + lks, writes=wkeys)
        return b0, ncol, wkeys

    def softmax(h, j, b0, ncol, wkeys, n):
        S = k.ps[:, b0 * 512:b0 * 512 + ncol]
        m, mk = mx[n % 2]
        P.op("dve", lambda e: e.tensor_reduce(out=m, in_=S, axis=AX.X, op=ALU.max, negate=True), reads=wkeys, writes=[mk])
        pb, pbk = Pb[n % 2]
        k.act(pb[:, 0:ncol], S, AF.Exp, wkeys + [mk], [pbk], bias=m)

    def tpv(h, j, ncol, n):
        b = h % 2
        nk = ncol // 128
        nA = (j + 1)
        pb, pbk = Pb[n % 2]
        pt, ptk = PT[n % 2]
        for r0 in range(0, nk, 8):
            r1 = min(r0 + 8, nk)
            tb, tbk = psT[state["pt"] % 2]
            state["pt"] += 1
            fns = [lambda e, i=i, tb=tb, r0=r0: e.transpose(tb[:, (i - r0) * 128:(i - r0 + 1) * 128], pb[:, i * 128:(i + 1) * 128],
                                                     k.ident_b) for i in range(r0, r1)]
            P.op("pe", fns, reads=[pbk, k.ibk], writes=[tbk])
            k.act(pt[:, r0 * 128:r1 * 128], tb[:, 0:(r1 - r0) * 128], AF.Copy, [tbk], [ptk + f"r{r0}"])

    def pv(h, j, ncol, n):
        b = h % 2
        nk = ncol // 128
        nA = (j + 1)
        pt, ptk = PT[n % 2]
        va, vk = Vh[b]
        v3 = va.rearrange("p (b f) -> p b f", b=16)
        fns = []
        for i in range(nk):
            kb = i if i < nA else 8 + (i - nA)
            fns.append(lambda e, i=i, kb=kb: e.matmul(psO[:, 0:129], lhsT=pt[:, i * 128:(i + 1) * 128], rhs=v3[:, kb, :],
                                                      start=(i == 0), stop=(i == nk - 1)))
        P.op("pe", fns, reads=[ptk + "r0", ptk + "r8", vk, vk + "o"], writes=[kO])
        r, rk = rs[n % 2]
        P.op("dve", lambda e: e.reciprocal(out=r, in_=psO[:, 128:129]), reads=[kO], writes=[rk])
        a, ak = att[n % 2]
        k.ts("dve", a, psO[:, 0:128], r, None, ALU.mult, None, [kO, rk], [ak])

    def tf(h, j, ncol, n):
        b = h % 2
        a, ak = att[n % 2]
        tb, tbk = psMb, kM
        P.op("pe", lambda e: e.transpose(tb[:, 0:128], a, k.ident_b), reads=[ak, k.ibk], writes=[tbk])
        ma, mk_ = mixh[b]
        k.act(ma[:, j * 128:(j + 1) * 128], tb[:, 0:128], AF.Copy, [tbk], [mk_])
        if j == NBLK - 1:
            k.dma("sp", mixT_d[h * 128:(h + 1) * 128, :], ma, [mk_], ["mixT_d"], f"mixst{b}")

    steps = [(h, j) for h in range(NH) for j in range(NBLK)]
    NS = len(steps)
    info = {}
    head_setup(0)
    info[0] = qk(*steps[0])
    for n in range(NS + 2):
        if n < NS:
            h, j = steps[n]
            b0, ncol, wkeys = info[n]
            softmax(h, j, b0, ncol, wkeys, n)
            if j == 2 and h + 1 < NH:
                head_setup(h + 1)
        if n + 1 < NS:
            info[n + 1] = qk(*steps[n + 1])
        if n < NS:
            tpv(*steps[n], info[n][1], n)
        if 0 <= n - 1 < NS:
            pv(*steps[n - 1], info[n - 1][1], n - 1)
        if 0 <= n - 2 < NS:
            tf(*steps[n - 2], info[n - 2][1], n - 2)


def phase_attn2(k, layer, cqn_d, lat_all, pos_d, mask_d, w_uq, w_ukv, mixT_d):
    P = k.P
    k.phase()
    rope_tables(k, pos_d)
    mb, mbk = k.alloc(NBLK * 256, BF16)
    k.dma("pool", mb, mask_d, [], [mbk], "maskb")
    mb3 = mb.rearrange("p (j c) -> p j c", j=NBLK)
    ones_b, obk = k.alloc(128, BF16, parts=1)
    P.op("dve", lambda e: e.tensor_copy(out=ones_b, in_=k.ones_f[0:1, :]), reads=[k.ck], writes=[obk])
    cqn, cqk = k.alloc(4 * T, BF16)
    cqn = cqn.rearrange("p (c t) -> p c t", c=4)
    k.dma("sp", cqn, cqn_d.rearrange("(c p) t -> p c t", p=128), ["cqn_d"], [cqk], "cqn")
    lat, _ = k.alloc(2 * 2048, BF16)
    lat = lat.rearrange("p (c t) -> p c t", c=2)
    kra, _ = k.alloc(2048, BF16, parts=65)
    P.op("pool", lambda e: e.memset(kra[64:65, :], 1.0), writes=["kra_ones"])
    onesb_col, ocbk = k.alloc(1, BF16)
    P.op("dve", lambda e: e.tensor_copy(out=onesb_col, in_=k.ones_f[:, 0:1]), reads=[k.ck], writes=[ocbk])
    lks = []
    for r in range(2):
        for kc in range(2):
            kk = f"lat{r}{kc}"
            k.dma("sp", lat[:, kc, r * 1024:(r + 1) * 1024], lat_all[r * 320 + kc * 128:r * 320 + (kc + 1) * 128, :],
                  ["lat_all"], [kk], kk)
            lks.append(kk)
        kk = f"kra{r}"
        k.dma("sp", kra[0:64, r * 1024:(r + 1) * 1024], lat_all[r * 320 + 256:r * 320 + 320, :], ["lat_all"], [kk], kk)
        lks.append(kk)
    wq = [k.alloc(4 * 192, BF16) for _ in range(2)]
    wkv = [k.alloc(2 * 256, BF16) for _ in range(2)]
    qn = [k.alloc(T, BF16) for _ in range(2)]
    qr = [k.alloc(T, BF16, parts=65) for _ in range(2)]
    qrf = [k.alloc(T, F32, parts=64)] * 2
    kT = [k.alloc(2048, BF16) for _ in range(2)]
    Vh = [k.alloc(16 * 129, BF16) for _ in range(2)]
    mixh = [k.alloc(T, BF16) for _ in range(2)]
    NPT = 2 * 36 * 128
    PT = [k.alloc(NPT, BF16) for _ in range(2)]
    negm = [k.alloc(T, BF16, parts=1) for _ in range(2)]
    att = [k.alloc(128, BF16) for _ in range(2)]
    rs = [k.alloc(1, F32) for _ in range(2)]
    sqt = [k.alloc(512, BF16) for _ in range(3)]
    q2row, q2k = k.alloc(T, F32, parts=1)
    k2row, k2k = k.alloc(512, F32, parts=1)
    kmax4, km4k = k.alloc(4, F32, parts=1)
    kr2row, kr2k = k.alloc(2048, F32, parts=1)
    kmax, kmk = k.alloc(1, F32, parts=1)
    for i in range(2):
        v3 = Vh[i][0].rearrange("p (b f) -> p b f", b=16)
        P.op("pool", lambda e, v3=v3: e.memset(v3[:, :, 128:129], 1.0), writes=[Vh[i][1] + "o"])
    qscr = rope_scratch(k, T)
    psR, kR = k.ps[:, 5 * 512:6 * 512], "ps5"
    psO, kO = k.ps[:, 6 * 512:7 * 512], "ps6"
    psM, kM = k.ps[:, 7 * 512:8 * 512], "ps7"
    psMb = psM.bitcast(BF16)
    sbanks = [(k.ps[:, 4 * 512:5 * 512], "ps4"), (psM, kM)]
    ones_col = k.ones_f[:, 0:1]
    st = {"bank": 0, "sq": 0, "sb": 0}

    def sbank():
        x_ = sbanks[st["sb"] % 2]
        st["sb"] += 1
        return x_
    bases = {}
    off = 0
    for r in range(2):
        for p in range(NBLK):
            bases[r * 8 + p] = off
            off += (NBLK - p) * 128

    def sq_next():
        a = sqt[st["sq"] % 3]
        st["sq"] += 1
        return a

    for t in range(4):
        tsl = slice(t * 512, (t + 1) * 512)
        s_, sk = sq_next()
        k.act(s_[0:64, :], kra[0:64, tsl], AF.Square, lks, [sk])
        k.mm(psR[0:1, :], [(onesb_col[0:64, :], s_[0:64, :])], [ocbk, sk], [kR])
        k.act(kr2row[:, tsl], psR[0:1, :], AF.Copy, [kR], [kr2k])

    def head_setup(h):
        b = h % 2
        pend = []

        def flush():
            while pend:
                pend.pop(0)()
        wqa, wqk = wq[b]
        wq3 = wqa.rearrange("p (c n) -> p c n", c=4)
        k.dma("pool", wq3, w_uq[:, h * 192:(h + 1) * 192].rearrange("(c p) n -> p c n", p=128), [], [wqk], f"wq{b}")
        wka, wkk = wkv[b]
        wk3 = wka.rearrange("p (c n) -> p c n", c=2)
        k.dma("pool", wk3, w_ukv[:, h * 256:(h + 1) * 256].rearrange("(c p) n -> p c n", p=128), [], [wkk], f"wkv{b}")
        yield
        qna, qnk = qn[b]
        qfa, qfk = qrf[b]
        for t in range(2):
            tsl = slice(t * 512, (t + 1) * 512)
            flush()
            pm, pmk = sbank()
            k.mm(pm, [(wq3[:, kc, 0:128], cqn[:, kc, tsl]) for kc in range(4)], [wqk, cqk], [pmk])
            k.act(qna[:, tsl], pm, AF.Copy, [pmk], [qnk], scale=SCALE)
            s1, s1k = sq_next()
            k.act(s1, pm, AF.Square, [pmk], [s1k], scale=SCALE)
            pm2, pm2k = sbank()
            k.mm(pm2[0:64, :], [(wq3[:, kc, 128:192], cqn[:, kc, tsl]) for kc in range(4)], [wqk, cqk], [pm2k])
            k.act(qfa[:, tsl], pm2[0:64, :], AF.Copy, [pm2k], [qfk])
            s2, s2k = sq_next()
            k.act(s2[0:64, :], pm2[0:64, :], AF.Square, [pm2k], [s2k], scale=SCALE)

            def stat_q(s1=s1, s1k=s1k, s2=s2, s2k=s2k, tsl=tsl):
                k.mm(psR[0:1, :], [(onesb_col, s1), (onesb_col[0:64, :], s2[0:64, :])], [ocbk, s1k, s2k], [kR])
                k.act(q2row[:, tsl], psR[0:1, :], AF.Copy, [kR], [q2k])
            pend.append(stat_q)
            yield
        rope_apply(k, qfa, qfk, qr[b][0][0:64, :], qr[b][1], slice(0, T), SCALE, "q", qscr)
        yield
        kTa, kTk = kT[b]
        for t in range(4):
            tsl = slice(t * 512, (t + 1) * 512)
            flush()
            pm, pmk = sbank()
            k.mm(pm, [(wk3[:, kc, 0:128], lat[:, kc, tsl]) for kc in range(2)], [wkk] + lks, [pmk])
            k.act(kTa[:, tsl], pm, AF.Copy, [pmk], [kTk])
            s1, s1k = sq_next()
            k.act(s1, pm, AF.Square, [pmk], [s1k])

            def stat_k(s1=s1, s1k=s1k, tsl=tsl, t=t):
                k.mm(psR[0:1, :], [(onesb_col, s1)], [ocbk, s1k], [kR])
                k.tt("dve", k2row, psR[0:1, :], kr2row[:, tsl], ALU.add, [kR, kr2k], [k2k])
                P.op("dve", lambda e, t=t: e.tensor_reduce(out=kmax4[:, t:t + 1], in_=k2row, axis=AX.X, op=ALU.max),
                     reads=[k2k], writes=[km4k])
            pend.append(stat_k)
            yield
        yield
        flush()
        P.op("dve", lambda e: e.tensor_reduce(out=kmax, in_=kmax4, axis=AX.X, op=ALU.max), reads=[km4k], writes=[kmk])
        nm, nmk = negm[b]
        k.ts("dve", kmax, kmax, -0.5 / BOUND_C, None, ALU.mult, None, [kmk], [kmk])
        k.ts("dve", q2row, q2row, -0.5 * BOUND_C, None, ALU.mult, None, [q2k], [q2k])
        k.ts("dve", nm, q2row, kmax, None, ALU.add, None, [q2k, kmk], [nmk])
        k.dma("sp", qr[b][0][64:65, :], nm, [nmk], [qr[b][1] + "m"], f"nmrow{b}")
        yield
        va, vk = Vh[b]
        v3 = va.rearrange("p (b f) -> p b f", b=16)
        for g in range(4):
            pm, pmk = sbank()
            fns = []
            for i in range(4):
                kb = g * 4 + i
                for kc in range(2):
                    fns.append(lambda e, i=i, kb=kb, kc=kc, pm=pm: e.matmul(pm[:, i * 128:(i + 1) * 128],
                                                                            lhsT=lat[:, kc, kb * 128:(kb + 1) * 128],
                                                                            rhs=wk3[:, kc, 128:256], start=(kc == 0), stop=(kc == 1)))
            P.op("pe", fns, reads=[wkk] + lks, writes=[pmk])
            P.op("dve", lambda e, g=g, v3=v3, pm=pm: e.tensor_copy(out=v3[:, g * 4:(g + 1) * 4, 0:128],
                                                                  in_=pm.rearrange("p (b f) -> p b f", b=4)),
                 reads=[pmk], writes=[vk])
            if g % 2 == 1:
                yield

    def qkt(h, p):
        b = h % 2
        pt, ptk = PT[b]
        q0 = p * 128
        for r in range(2):
            i = r * 8 + p
            kcols = slice(r * 1024 + p * 128, r * 1024 + (p + 1) * 128)
            for c0 in range(q0, T, 512):
                c1 = min(c0 + 512, T)
                n = c1 - c0
                bk = st["bank"] % 4
                st["bank"] += 1
                ps, pk = k.ps[:, bk * 512:bk * 512 + n], f"ps{bk}"
                diag = (c0 == q0)
                fns = [lambda e, ps=ps, kcols=kcols, c0=c0, c1=c1: e.matmul(ps, lhsT=kT[b][0][:, kcols], rhs=qn[b][0][:, c0:c1],
                                                                           start=True, stop=False),
                       lambda e, ps=ps, kcols=kcols, c0=c0, c1=c1, diag=diag: e.matmul(
                           ps, lhsT=kra[0:65, kcols], rhs=qr[b][0][0:65, c0:c1], start=False, stop=(not diag))]
                if diag:
                    fns.append(lambda e, ps=ps, r=r: e.matmul(ps[:, 0:128], lhsT=mb3[:, p, r * 128:(r + 1) * 128], rhs=k.ident_b,
                                                              start=False, stop=True))
                P.op("pe", fns, reads=[kT[b][1], qn[b][1], qr[b][1], qr[b][1] + "m", "kra_ones", mbk, k.ibk] + lks, writes=[pk])
                o0 = bases[i] + (c0 - q0)
                k.act(pt[:, o0:o0 + n], ps, AF.Exp, [pk], [ptk + f"_{i}"])

    def pvj(h, j):
        b = h % 2
        pt, ptk = PT[b]
        va, vk = Vh[b]
        v3 = va.rearrange("p (b f) -> p b f", b=16)
        blocks = [r * 8 + p for r in range(2) for p in range(j + 1)]
        fns = []
        for x, i in enumerate(blocks):
            p = i % 8
            o0 = bases[i] + (j - p) * 128
            fns.append(lambda e, x=x, i=i, o0=o0: e.matmul(psO[:, 0:129], lhsT=pt[:, o0:o0 + 128], rhs=v3[:, i, :],
                                                           start=(x == 0), stop=(x == len(blocks) - 1)))
        P.op("pe", fns, reads=[ptk + f"_{i}" for i in blocks] + [vk, vk + "o"], writes=[kO])
        n = h * NBLK + j
        r_, rk = rs[n % 2]
        P.op("dve", lambda e: e.reciprocal(out=r_, in_=psO[:, 128:129]), reads=[kO], writes=[rk])
        a, ak = att[n % 2]
        k.ts("dve", a, psO[:, 0:128], r_, None, ALU.mult, None, [kO, rk], [ak])

    def tf(h, j):
        b = h % 2
        n = h * NBLK + j
        a, ak = att[n % 2]
        P.op("pe", lambda e: e.transpose(psMb[:, 0:128], a, k.ident_b), reads=[ak, k.ibk], writes=[kM])
        ma, mk_ = mixh[b]
        k.act(ma[:, j * 128:(j + 1) * 128], psMb[:, 0:128], AF.Copy, [kM], [mk_])
        if j == NBLK - 1:
            k.dma("sp", mixT_d[h * 128:(h + 1) * 128, :], ma, [mk_], ["mixT_d"], f"mixst{b}")

    steps = [(h, p) for h in range(NH) for p in range(NBLK)]
    NS = len(steps)
    for _ in head_setup(0):
        pass
    gen = None
    qkt(*steps[0])
    for n in range(NS + 1):
        if n < NS:
            h, p = steps[n]
            if p == 0 and h + 1 < NH:
                gen = head_setup(h + 1)
            if gen is not None:
                for _ in range(2):
                    try:
                        next(gen)
                    except StopIteration:
                        gen = None
                        break
            if p == NBLK - 2 and gen is not None:
                for _ in gen:
                    pass
                gen = None
        if n + 1 < NS:
            qkt(*steps[n + 1])
        if n < NS:
            pvj(*steps[n])
        if 0 <= n - 1 < NS:
            tf(*steps[n - 1])


def phase_outproj(k, layer, mixT_d, XF_d, xfk, w_out, y_d, XF_o, xfok, XT_o, xtok):
    P = k.P
    k.phase()
    mt, _ = k.alloc(16 * T, BF16)
    mt = mt.rearrange("p (c t) -> p c t", c=16)
    mks = []
    for q in range(4):
        kk = f"xtq{q}"
        k.dma("sp", mt[:, q * 4:(q + 1) * 4, :], mixT_d[q * 512:(q + 1) * 512, :].rearrange("(c p) t -> p c t", p=128),
              ["mixT_d"], [kk], kk)
        mks.append(kk)
    k.ring_init(4, 8192)
    xf = [k.alloc(512, F32) for _ in range(3)]
    yres, _ = k.alloc(16 * T, F32)
    yres = yres.rearrange("p (c t) -> p c t", c=16)
    yks = [f"yres{c}" for c in range(16)]
    n = 0
    for pn in range(4):
        pan, pkey = k.wload(w_out[:, pn * 512:(pn + 1) * 512], 16, 512)
        for o4 in range(4):
            oc = pn * 4 + o4
            for t in range(2):
                tsl = slice(t * 512, (t + 1) * 512)
                ps, pk = k.bank()
                k.mm(ps, [(pan[:, kc, o4 * 128:(o4 + 1) * 128], mt[:, kc, tsl]) for kc in range(16)], [pkey] + mks, [pk])
                x, xk = xf[n % 3]
                k.dma("sp", x, XF_d[oc * 128:(oc + 1) * 128, tsl], [xfk], [xk], f"opx{n%3}")
                k.stt(yres[:, oc, tsl], x, ALPHA, ps, ALU.mult, ALU.add, [xk, pk], [yks[oc]])
                n += 1
    lnfm(k, None, None, 16, "ln1_g", "ln1_b", layer, LN_EPS, AF.Identity, dst_f=XF_o, dfk=xfok, dst_b=XT_o, dbk=xtok,
         res=(yres, yks))


def phase_ffn(k, layer, XF_d, xfk, XT_d, xtk, w1, w2, y_d, XF_o, xfok, XT_o, xtok):
    P = k.P
    k.phase()
    acc, _ = k.alloc(16 * T, F32)
    acc = acc.rearrange("p (c t) -> p c t", c=16)
    mark = k.off
    xt, _ = k.alloc(16 * T, BF16)
    xt = xt.rearrange("p (c t) -> p c t", c=16)
    xks = []
    for q in range(4):
        kk = f"xtq{q}"
        k.dma("sp", xt[:, q * 4:(q + 1) * 4, :], XT_d[q * 512:(q + 1) * 512, :].rearrange("(c p) t -> p c t", p=128),
              [xtk], [kk], kk)
        xks.append(kk)
    aks = []
    for c in range(16):
        kk = f"acc{c}"
        k.dma("sp", acc[:, c, :], XF_d[c * 128:(c + 1) * 128, :], [xfk], [kk], kk)
        k.act(acc[:, c, :], acc[:, c, :], AF.Copy, [kk], [kk], scale=ALPHA)
        aks.append(kk)
    k.ring_init(4, 8192)
    h1 = [k.alloc(4 * T, BF16) for _ in range(2)]
    rt = [k.alloc(512, F32) for _ in range(2)]
    n = 0
    NG = DFF // 512
    for g in range(NG):
        p1, p1k = k.wload(w1[:, g * 512:(g + 1) * 512], 16, 512)
        p2, p2k = k.wload(w2[g * 512:(g + 1) * 512, :], 4, 2048)
        ha, hk = h1[g % 2]
        h3 = ha.rearrange("p (c t) -> p c t", c=4)
        for fc in range(4):
            for t in range(2):
                tsl = slice(t * 512, (t + 1) * 512)
                ps, pk = k.bank()
                k.mm(ps, [(p1[:, kc, fc * 128:(fc + 1) * 128], xt[:, kc, tsl]) for kc in range(16)], [p1k] + xks, [pk])
                r, rk = rt[n % 2]
                k.act(r, ps, AF.Relu, [pk], [rk])
                k.act(h3[:, fc, tsl], r, AF.Square, [rk], [hk + f"_{fc}_{t}"])
                n += 1
        hkeys = [hk + f"_{fc}_{t}" for fc in range(4) for t in range(2)]
        for oc in range(16):
            for t in range(2):
                tsl = slice(t * 512, (t + 1) * 512)
                ps, pk = k.bank()
                k.mm(ps, [(p2[:, fc, oc * 128:(oc + 1) * 128], h3[:, fc, tsl]) for fc in range(4)], [p2k] + hkeys, [pk])
                k.tt("dve", acc[:, oc, tsl], acc[:, oc, tsl], ps, ALU.add, [aks[oc], pk], [aks[oc]])
    k.P.barrier()
    k.off = mark
    lnfm(k, None, None, 16, "ln2_g", "ln2_b", layer, LN_EPS, AF.Identity, dst_f=XF_o, dfk=xfok, dst_b=XT_o, dbk=xtok,
         res=(acc, aks))


def phase_out_ffn(k, layer, mixT_d, XF_d, xfk, w_out, w1, w2, XF_o, xfok):
    P = k.P
    k.phase()
    X, _ = k.alloc(16 * T, BF16)
    X = X.rearrange("p (c t) -> p c t", c=16)
    acc, _ = k.alloc(16 * T, F32)
    acc = acc.rearrange("p (c t) -> p c t", c=16)
    aks = [f"acc{c}" for c in range(16)]
    scr = ln_scratch(k)
    gbs, gbk = k.alloc(32, F32)
    k.ts("dve", gbs, k.vecs[:, vcol_idx(k, "ln1_g", layer):vcol_idx(k, "ln1_g", layer) + 32], ALPHA, None, ALU.mult, None,
         [k.vk], [gbk])
    mark = k.off
    mks = []
    for q in range(4):
        kk = f"xtq{q}"
        k.dma("sp", X[:, q * 4:(q + 1) * 4, :], mixT_d[q * 512:(q + 1) * 512, :].rearrange("(c p) t -> p c t", p=128),
              ["mixT_d"], [kk], kk)
        mks.append(kk)
    xf = [k.alloc(512, F32) for _ in range(3)]
    n = 0
    for pn in range(4):
        pan, pkey = k.wload(w_out[:, pn * 512:(pn + 1) * 512], 16, 512)
        for o4 in range(4):
            oc = pn * 4 + o4
            for t in range(2):
                tsl = slice(t * 512, (t + 1) * 512)
                ps, pk = k.bank()
                k.mm(ps, [(pan[:, kc, o4 * 128:(o4 + 1) * 128], X[:, kc, tsl]) for kc in range(16)], [pkey] + mks, [pk])
                x, xk = xf[n % 3]
                k.dma("sp", x, XF_d[oc * 128:(oc + 1) * 128, tsl], [xfk], [xk], f"opx{n%3}")
                k.stt(acc[:, oc, tsl], x, ALPHA, ps, ALU.mult, ALU.add, [xk, pk], [aks[oc]])
                n += 1

    def chunk(c, t):
        return acc[:, c, t * 512:(t + 1) * 512], aks[c]

    stats = ln_stats(k, chunk, 16, LN_EPS, scr)
    P.barrier()
    k.off = mark
    xkeys = [f"X{c}" for c in range(16)]

    def epi1(t, c0, grp):
        tsl = slice(t * 512, (t + 1) * 512)
        for i, (x, xk) in enumerate(grp):
            c = c0 + i
            k.act(X[:, c, tsl], x, AF.Identity, [xk, k.vk], [xkeys[c]], scale=vcol(k, "ln1_g", layer, c),
                  bias=vcol(k, "ln1_b", layer, c))

            if c % 2 == 1:
                k.act(x, x, AF.Identity, [xk, gbk], [xk], scale=gbs[:, c:c + 1], bias=gbs[:, 16 + c:17 + c])

        def post(c0=c0, grp=grp):
            for i, (x, xk) in enumerate(grp):
                c = c0 + i
                if c % 2 == 0:
                    k.ts("dve", x, x, gbs[:, c:c + 1], gbs[:, 16 + c:17 + c], ALU.mult, ALU.add, [xk, gbk], [xk])
        return post

    ln_norm(k, chunk, stats, 16, epi1)
    h1 = [k.alloc(4 * T, BF16) for _ in range(2)]
    rt = [k.alloc(512, F32) for _ in range(2)]
    n = 0
    NG = DFF // 512
    for g in range(NG):
        p1, p1k = k.wload(w1[:, g * 512:(g + 1) * 512], 16, 512)
        p2, p2k = k.wload(w2[g * 512:(g + 1) * 512, :], 4, 2048)
        ha, hk = h1[g % 2]
        h3 = ha.rearrange("p (c t) -> p c t", c=4)
        for fc in range(4):
            for t in range(2):
                tsl = slice(t * 512, (t + 1) * 512)
                ps, pk = k.bank()
                k.mm(ps, [(p1[:, kc, fc * 128:(fc + 1) * 128], X[:, kc, tsl]) for kc in range(16)], [p1k] + xkeys, [pk])
                r, rk = rt[n % 2]
                k.act(r, ps, AF.Relu, [pk], [rk])
                k.act(h3[:, fc, tsl], r, AF.Square, [rk], [hk + f"_{fc}_{t}"])
                n += 1
        hkeys = [hk + f"_{fc}_{t}" for fc in range(4) for t in range(2)]
        for oc in range(16):
            for t in range(2):
                tsl = slice(t * 512, (t + 1) * 512)
                ps, pk = k.bank()
                k.mm(ps, [(p2[:, fc, oc * 128:(oc + 1) * 128], h3[:, fc, tsl]) for fc in range(4)], [p2k] + hkeys, [pk])
                k.tt("dve", acc[:, oc, tsl], acc[:, oc, tsl], ps, ALU.add, [aks[oc], pk], [aks[oc]])
    stats = ln_stats(k, chunk, 16, LN_EPS, scr)
    st = {"n": 0}

    def epi2(t, c0, grp):
        tsl = slice(t * 512, (t + 1) * 512)
        for i, (x, xk) in enumerate(grp):
            c = c0 + i
            k.act(X[:, c, tsl], x, AF.Identity, [xk, k.vk], [xkeys[c]], scale=vcol(k, "ln2_g", layer, c),
                  bias=vcol(k, "ln2_b", layer, c))

            if c % 2 == 1:
                k.act(x, x, AF.Identity, [xk, k.vk], [xk], scale=vcol(k, "ln2_g", layer, c), bias=vcol(k, "ln2_b", layer, c))

        def post(t=t, c0=c0, grp=grp, tsl=tsl):
            for i, (x, xk) in enumerate(grp):
                c = c0 + i
                if c % 2 == 0:
                    k.ts("dve", x, x, vcol(k, "ln2_g", layer, c), vcol(k, "ln2_b", layer, c), ALU.mult, ALU.add,
                         [xk, k.vk], [xk])
            k.dma("sp", XF_o[c0 * 128:(c0 + 4) * 128, tsl].rearrange("(c p) n -> p c n", p=128), acc[:, c0:c0 + 4, tsl],
                  [xk for (_, xk) in grp], [xfok], f"lnst{st['n']%4}")
            st["n"] += 1
        return post

    ln_norm(k, chunk, stats, 16, epi2)


def wsched_a(w_in):
    sc = [(w_in[:, 0:512], 16, 512), (w_in[:, 512:832], 16, 320)]
    for hp in range(2):
        sc.append((w_in[:, 832 + hp * 512:832 + (hp + 1) * 512], 16, 512))
        sc.append((w_in[:, 832 + 1024 + hp * 512:832 + 1024 + (hp + 1) * 512], 16, 512))
    return sc


def wsched_b(w_out, w1, w2):
    sc = [(w_out[:, pn * 512:(pn + 1) * 512], 16, 512) for pn in range(4)]
    for g in range(DFF // 512):
        sc.append((w1[:, g * 512:(g + 1) * 512], 16, 512))
        sc.append((w2[g * 512:(g + 1) * 512, :], 4, 2048))
    return sc


ATTN = phase_attn2
PAIRS = [[0, 1], [2, 3], [4, 5], [6, 7]]
DEBUG = False


def build(kind):
    nc = bass.Bass("TRN2", target_bir_lowering=False)

    def din(name, shape, dt=F32):
        return nc.dram_tensor(name, shape, dt, kind="ExternalInput").ap()

    def dout(name, shape, dt=F32):
        return nc.dram_tensor(name, shape, dt, kind="ExternalOutput").ap()

    def dint(name, shape, dt=F32):
        return nc.dram_tensor(name, shape, dt).ap()

    es = ExitStack()
    with es:
        k = K(nc, es, kind)
        nv = NV if kind == "FUSED" else NVG + NVL
        vecs_d = din("vecs", [128, nv])
        cst_d = din("cst", [128, 256])
        load_consts(k, vecs_d, cst_d, nv)
        outs = []
        if kind == "P0":
            xT = din("xT", [D, T])
            XF = dout("XF_o", [D, T])
            XT_ = dout("XT_o", [D, T], BF16)
            k.phase()
            lnfm(k, xT, "xT", 16, "ln_in_g", "ln_in_b", 0, LN_EPS, AF.Identity, dst_f=XF, dfk="XF_o", dst_b=XT_, dbk="XT_o")
            outs = ["XF_o", "XT_o"]
            k.ring_setup([])
        elif kind == "A":
            XT_ = din("XT_i", [D, T], BF16)
            pos = din("pos", [64, T], I32)
            w_in = din("w_in", [D, IN_COLS])
            cqn = dout("cqn_d", [QL, T], BF16)
            lat = dout("lat_own", [320, T], BF16)
            h_d = dout("h_d", [CW, T])
            halo = dout("halo_own", [CW, 256])
            k.ring_setup(wsched_a(w_in))
            k.base = k.off
            rope_tables(k, pos)
            phase_a(k, 0, XT_, "XT_i", pos, w_in, cqn, lat, h_d, halo)
            outs = ["cqn_d", "lat_own", "h_d", "halo_own"]
        elif kind == "B":
            XF = din("XF_i", [D, T])
            cqn = din("cqn_d", [QL, T], BF16)
            lat_all = din("lat_all", [640, T], BF16)
            h_d = din("h_d", [CW, T])
            halo_all = din("halo_all", [2 * CW, 256])
            pos = din("pos", [64, T], I32)
            maskf = din("maskf", [128, NBLK * 256])
            w_uq = din("w_uq", [QL, NH * 192])
            w_ukv = din("w_ukv", [KVL, NH * 256])
            w_out = din("w_out", [D, D])
            w1 = din("w1", [D, DFF])
            w2 = din("w2", [DFF, D])
            dbg = dout if DEBUG else dint
            yc_d = dbg("yc_d", [CW, T])
            mixT = dbg("mixT_d", [D, T], BF16)
            y_d = dint("y_d", [D, T])
            XF1 = dbg("XF_1", [D, T])
            XT1 = dint("XT_1", [D, T], BF16)
            XFo = dout("XF_o", [D, T])
            XTo = dout("XT_o", [D, T], BF16)
            k.ring_setup(wsched_b(w_out, w1, w2))
            k.base = k.off
            rope_tables(k, pos)
            phase_conv(k, 0, h_d, halo_all, yc_d, mixT)
            ATTN(k, 0, cqn, lat_all, pos, maskf, w_uq, w_ukv, mixT)
            phase_outproj(k, 0, mixT, XF, "XF_i", w_out, y_d, XF1, "XF_1", XT1, "XT_1")
            phase_ffn(k, 0, XF1, "XF_1", XT1, "XT_1", w1, w2, y_d, XFo, "XF_o", XTo, "XT_o")
            outs = ["XF_o", "XT_o"]
        elif kind == "FUSED":
            xT = din("xT", [D, T])
            pos = din("pos", [64, T], I32)
            maskf = din("maskf", [128, NBLK * 256])
            w_in = din("w_in", [DEPTH, D, IN_COLS])
            w_uq = din("w_uq", [DEPTH, QL, NH * 192])
            w_ukv = din("w_ukv", [DEPTH, KVL, NH * 256])
            w_out = din("w_out", [DEPTH, D, D])
            w1 = din("w1", [DEPTH, D, DFF])
            w2 = din("w2", [DEPTH, DFF, D])
            XF = [dint("XF_a", [D, T]), dint("XF_b", [D, T])]
            XT_ = [dint("XT_a", [D, T], BF16), dint("XT_b", [D, T], BF16)]
            cqn = dint("cqn_d", [QL, T], BF16)
            lat = dint("lat_own", [320, T], BF16)
            lat_all = dint("lat_all", [640, T], BF16)
            h_d = dint("h_d", [CW, T])
            halo = dint("halo_own", [CW, 256])
            halo_all = dint("halo_all", [2 * CW, 256])
            yc_d = dint("yc_d", [CW, T])
            mixT = dint("mixT_d", [D, T], BF16)
            y_d = dint("y_d", [D, T])
            outT = dout("outT", [D, T])
            sched = []
            for l in range(DEPTH):
                sched += wsched_a(w_in[l]) + wsched_b(w_out[l], w1[l], w2[l])
            k.ring_setup(sched)
            k.base = k.off
            for _ in range(3):
                k._wissue()
            rope_tables(k, pos)
            k.phase()
            X0, _ = k.alloc(16 * T, BF16)
            X0 = X0.rearrange("p (c t) -> p c t", c=16)
            scr = ln_scratch(k)
            xs = [[k.alloc(4 * 512, F32) for _ in range(4)] for _ in range(2)]

            def chunk0(c, t):
                a_, ak_ = xs[t][c // 4]
                return a_[:, (c % 4) * 512:(c % 4 + 1) * 512], ak_

            def loader0(c, t):
                if c % 4 == 0:
                    k.dma("sp", xs[t][c // 4][0].rearrange("p (c n) -> p c n", c=4),
                          xT[c * 128:(c + 4) * 128, t * 512:(t + 1) * 512].rearrange("(c p) n -> p c n", p=128),
                          [], [xs[t][c // 4][1]], f"lnx{t}_{c//4}")

            stats0 = ln_stats(k, chunk0, 16, LN_EPS, scr, loader=loader0)
            cnt = {"n": 0}

            def epi0(t, c0, grp):
                tsl = slice(t * 512, (t + 1) * 512)
                for i, (x, xk) in enumerate(grp):
                    c = c0 + i
                    k.act(X0[:, c, tsl], x, AF.Identity, [xk, k.vk], [f"X{c}"], scale=vcol(k, "ln_in_g", 0, c),
                          bias=vcol(k, "ln_in_b", 0, c))

                    if c % 2 == 1:
                        k.act(x, x, AF.Identity, [xk, k.vk], [xk], scale=vcol(k, "ln_in_g", 0, c), bias=vcol(k, "ln_in_b", 0, c))

                def post(t=t, c0=c0, grp=grp, tsl=tsl):
                    for i, (x, xk) in enumerate(grp):
                        c = c0 + i
                        if c % 2 == 0:
                            k.ts("dve", x, x, vcol(k, "ln_in_g", 0, c), vcol(k, "ln_in_b", 0, c), ALU.mult, ALU.add,
                                 [xk, k.vk], [xk])
                    k.dma("sp", XF[0][c0 * 128:(c0 + 4) * 128, tsl].rearrange("(c p) n -> p c n", p=128),
                          xs[t][c0 // 4][0].rearrange("p (c n) -> p c n", c=4), [xs[t][c0 // 4][1]], ["XF_a"],
                          f"lnst{cnt['n']%4}")
                    cnt["n"] += 1
                return post

            ln_norm(k, chunk0, stats0, 16, epi0)
            for l in range(DEPTH):
                phase_a(k, l, None, None, pos, w_in[l], cqn, lat, h_d, halo, x_in_sbuf=True)
                k.P.op("pool", lambda e: e.collective_compute("AllGather", ALU.bypass, replica_groups=PAIRS,
                                                              ins=[lat], outs=[lat_all]),
                       reads=["lat_own"], writes=["lat_all"], dma="cc_lat", inc=1)
                k.P.op("pool", lambda e: e.collective_compute("AllGather", ALU.bypass, replica_groups=PAIRS,
                                                              ins=[halo], outs=[halo_all]),
                       reads=["halo_own"], writes=["halo_all"], dma="cc_halo", inc=1)
                phase_conv(k, l, h_d, halo_all, yc_d, mixT)
                ATTN(k, l, cqn, lat_all, pos, maskf, w_uq[l], w_ukv[l], mixT)
                last = (l == DEPTH - 1)
                phase_out_ffn(k, l, mixT, XF[0], "XF_a", w_out[l], w1[l], w2[l],
                              outT if last else XF[0], "outT" if last else "XF_a")
            outs = ["outT"]
        k.P.barrier(final=True)
        k.P.emit()
    return nc


def _cols(v):
    v = np.asarray(v, np.float32).reshape(-1)
    return v.reshape(-1, 128).T


def pack_vecs(inp, rank, layers):
    cols = [_cols(inp["ln_in_g"]), _cols(inp["ln_in_b"])]
    invf = (10000.0 ** (-np.arange(0, ROPE, 2, dtype=np.float32) / ROPE)).astype(np.float32)
    c = np.zeros((128, 1), np.float32)
    c[0:32, 0] = invf
    c[32:64, 0] = invf
    cols.append(c)
    s = np.ones((128, 1), np.float32)
    s[0:32] = -1.0
    cols.append(s)
    cols.append(np.full((128, 1), 1.0 if rank == 0 else 0.0, np.float32))
    cols.append(np.full((128, 1), 1.0 if rank == 1 else 0.0, np.float32))
    for l in layers:
        cols.append(_cols(inp["g_q"][l]))
        cols.append(_cols(inp["g_kv"][l]))
        cols.append(_cols(inp["b_glu"][l]))
        wd = np.asarray(inp["w_dw"][l], np.float32)
        w = np.zeros((128, 8 * CK), np.float32)
        for cc in range(8):
            w[:, cc * CK:(cc + 1) * CK] = wd[:, cc * 128:(cc + 1) * 128].T
        cols.append(w)
        for nme in ["b_dw", "g_cln", "b_cln", "ln1_g", "ln1_b", "ln2_g", "ln2_b"]:
            cols.append(_cols(inp[nme][l]))
    return np.ascontiguousarray(np.concatenate(cols, axis=1))


def make_mask(rank):
    m = np.zeros((128, NBLK, 256), np.float32)
    diag = np.zeros((128, 128), np.float32)
    diag[0:64, 64:128] = -1e30
    own = slice(0, 128) if rank == 0 else slice(128, 256)
    oth = slice(128, 256) if rank == 0 else slice(0, 128)
    for j in range(NBLK):
        m[:, j, own] = diag
        m[:, j, oth] = -1e30 if rank == 0 else 0.0
    return np.ascontiguousarray(m.reshape(128, NBLK * 256))


def make_cst():
    c = np.zeros((128, 256), np.float32)
    c[:, 0:128] = np.eye(128, dtype=np.float32)
    c[:, 128:256] = 1.0
    return c


_PROGS = {}


def _prog(kind):
    if kind not in _PROGS:
        _PROGS[kind] = build(kind)
    return _PROGS[kind]


def _core_tokens(x, positions, c):
    b, r = c // 2, c % 2
    idx = np.concatenate([np.arange((2 * j + r) * 128, (2 * j + r + 1) * 128) for j in range(NBLK)])
    xT = np.ascontiguousarray(x[b][idx].T)
    pos = np.ascontiguousarray(np.broadcast_to(positions[b][idx].astype(np.int32)[None, :], (64, T)))
    return idx, xT, pos


FUSED = True


def kernel(**inp):
    inp = {k_: np.asarray(v) for k_, v in inp.items()}
    x = inp["x"].astype(np.float32, copy=False)
    positions = inp["positions"]
    cores = list(range(8))
    idxs, xTs, poss = zip(*[_core_tokens(x, positions, c) for c in cores])
    cst = make_cst()
    masks = [make_mask(c % 2) for c in cores]
    out = np.zeros((4, 2048, D), np.float32)
    if FUSED:
        ims = []
        for c in cores:
            ims.append({"vecs": pack_vecs(inp, c % 2, range(DEPTH)), "cst": cst, "xT": xTs[c], "pos": poss[c],
                        "maskf": masks[c], "w_in": inp["w_in"], "w_uq": inp["w_uq"], "w_ukv": inp["w_ukv"],
                        "w_out": inp["w_out"], "w1": inp["w1"], "w2": inp["w2"]})
        res = run_bass_kernel_spmd(_prog("FUSED"), ims, core_ids=cores).results
        for c in cores:
            out[c // 2][idxs[c]] = res[c]["outT"].T
        return out
    ims = [{"vecs": pack_vecs(inp, c % 2, [0]), "cst": cst, "xT": xTs[c]} for c in cores]
    res = run_bass_kernel_spmd(_prog("P0"), ims, core_ids=cores).results
    XF = [r["XF_o"] for r in res]
    XT_ = [r["XT_o"] for r in res]
    for l in range(DEPTH):
        vl = [pack_vecs(inp, c % 2, [l]) for c in cores]
        ims = [{"vecs": vl[c], "cst": cst, "XT_i": XT_[c], "pos": poss[c], "w_in": inp["w_in"][l]} for c in cores]
        ra = run_bass_kernel_spmd(_prog("A"), ims, core_ids=cores).results
        ims = []
        for c in cores:
            p0, p1 = (c // 2) * 2, (c // 2) * 2 + 1
            ims.append({"vecs": vl[c], "cst": cst, "XF_i": XF[c], "cqn_d": ra[c]["cqn_d"],
                        "lat_all": np.concatenate([ra[p0]["lat_own"], ra[p1]["lat_own"]], 0),
                        "h_d": ra[c]["h_d"],
                        "halo_all": np.concatenate([ra[p0]["halo_own"], ra[p1]["halo_own"]], 0),
                        "pos": poss[c], "maskf": masks[c], "w_uq": inp["w_uq"][l], "w_ukv": inp["w_ukv"][l],
                        "w_out": inp["w_out"][l], "w1": inp["w1"][l], "w2": inp["w2"][l]})
        rb = run_bass_kernel_spmd(_prog("B"), ims, core_ids=cores).results
        XF = [r["XF_o"] for r in rb]
        XT_ = [r["XT_o"] for r in rb]
    for c in cores:
        out[c // 2][idxs[c]] = XF[c].T
    return out
```
